# Optimizing a Trainium2 kernel written in Bass

```python
import jax, jax.numpy as jnp
from jax import lax
import numpy as np

D_MODEL = 1024
BATCH = 2
SEQ = 8192
DEPTH = 1
DEC_BATCH = 2
DEC_SEQ = 16384
PAST_LEN = 128

HEAD_DIM = 64
DILATED_PAIRS = ((128, 1), (512, 4), (2048, 16))
A_HEADS_PER_GROUP = 4
A_GROUPS = len(DILATED_PAIRS)
A_HEADS = A_HEADS_PER_GROUP * A_GROUPS
A_OUT = A_HEADS_PER_GROUP * HEAD_DIM
B_HEADS = 8
A_WIDTH = A_HEADS * HEAD_DIM
B_WIDTH = B_HEADS * HEAD_DIM
GATE_WIDTH = 2 * D_MODEL
IN_WIDTH = 3 * A_WIDTH + 3 * B_WIDTH + GATE_WIDTH
GRID_W = 64
NA_ROWS = 8
NA_COLS = 16
ROPE_THETA = 500000.0
ROT_DIM = HEAD_DIM // 4
D_FF = 2816
RMS_EPS = 1e-6
NEG = -1e30

kernel_name = "hybrid_dilated_neighbourhood_encoder"


def rms_norm(x, g):
    xf = x.astype(jnp.float32)
    y = xf * lax.rsqrt(jnp.mean(xf * xf, axis=-1, keepdims=True) + RMS_EPS)
    return (y * g.astype(jnp.float32)).astype(x.dtype)


def swiglu(h, w_gate, w_up, w_down):
    return (jax.nn.silu(h @ w_gate) * (h @ w_up)) @ w_down


def partial_rotary(x):
    S = x.shape[1]
    half = ROT_DIM // 2
    inv_freq = ROPE_THETA ** (-jnp.arange(0, ROT_DIM, 2, dtype=jnp.float32) / ROT_DIM)
    ang = jnp.arange(S, dtype=jnp.float32)[:, None] * inv_freq[None, :]
    cos = jnp.cos(ang)[None, :, None, :]
    sin = jnp.sin(ang)[None, :, None, :]
    xr = x[..., :ROT_DIM].astype(jnp.float32)
    x1, x2 = xr[..., :half], xr[..., half:]
    rot = jnp.concatenate([x1 * cos - x2 * sin, x2 * cos + x1 * sin], axis=-1).astype(x.dtype)
    return jnp.concatenate([rot, x[..., ROT_DIM:]], axis=-1)


def dilated_window_group(q, k, v, window, dilation):
    B, S, H, E = q.shape
    R = window // (2 * dilation)
    L = S // dilation
    nb = -(-L // R)
    Lp = nb * R

    def to_sub(x):
        return x.reshape(B, L, dilation, H, E).transpose(0, 2, 1, 3, 4)

    qb = jnp.pad(to_sub(q), ((0, 0), (0, 0), (0, Lp - L), (0, 0), (0, 0)))
    qb = qb.reshape(B, dilation, nb, R, H, E)

    def windows(x):
        xp = jnp.pad(to_sub(x), ((0, 0), (0, 0), (R, Lp - L + R), (0, 0), (0, 0)))
        xb = xp.reshape(B, dilation, nb + 2, R, H, E)
        return jnp.concatenate([xb[:, :, :-2], xb[:, :, 1:-1], xb[:, :, 2:]], axis=3)

    kw = windows(k)
    vw = windows(v)
    s = jnp.einsum('bdnqhe,bdnkhe->bdnhqk', qb, kw).astype(jnp.float32) * (E ** -0.5)
    qpos = (jnp.arange(nb) * R)[:, None] + jnp.arange(R)[None, :]
    kpos = (jnp.arange(nb) * R - R)[:, None] + jnp.arange(3 * R)[None, :]
    valid = (jnp.abs(kpos[:, None, :] - qpos[:, :, None]) <= R) \
        & (kpos >= 0)[:, None, :] & (kpos < L)[:, None, :]
    s = jnp.where(valid[None, None, :, None], s, NEG)
    lse = jax.nn.logsumexp(s, axis=-1)
    p = jnp.exp(s - lse[..., None])
    o = jnp.einsum('bdnhqk,bdnkhe->bdnqhe', p.astype(v.dtype), vw)
    o = o.reshape(B, dilation, Lp, H, E)[:, :, :L]
    o = o.transpose(0, 2, 1, 3, 4).reshape(B, S, H, E)
    lse = lse.transpose(0, 1, 2, 4, 3).reshape(B, dilation, Lp, H)[:, :, :L]
    lse = lse.transpose(0, 2, 1, 3).reshape(B, S, H)
    return o, lse


def neighbourhood_attention(q, k, v, rpb):
    B, S, H, E = q.shape
    rows = S // GRID_W
    kh = min(NA_ROWS, rows)
    qg = q.reshape(B, rows, GRID_W, H, E)
    kg = k.reshape(B, rows, GRID_W, H, E)
    vg = v.reshape(B, rows, GRID_W, H, E)
    ri = jnp.arange(rows)
    rs = jnp.clip(ri - kh // 2, 0, rows - kh)
    row_idx = rs[:, None] + jnp.arange(kh)[None, :]
    k_rows = kg[:, row_idx]
    v_rows = vg[:, row_idx]
    cj = jnp.arange(GRID_W)
    cs = jnp.clip(cj - NA_COLS // 2, 0, GRID_W - NA_COLS)
    col_valid = (cj[None, :] >= cs[:, None]) & (cj[None, :] < cs[:, None] + NA_COLS)
    dr_idx = row_idx - ri[:, None] + (NA_ROWS - 1)
    dc_idx = jnp.clip(cj[None, :] - cj[:, None] + (NA_COLS - 1), 0, 2 * NA_COLS - 2)
    bias = rpb[:, dr_idx[:, None, :, None], dc_idx[None, :, None, :]]
    bias = bias.transpose(1, 0, 2, 3, 4).astype(jnp.float32)
    s = jnp.einsum('brqhe,brakhe->brhqak', qg, k_rows).astype(jnp.float32) * (E ** -0.5)
    s = jnp.where(col_valid[:, None, :], s + bias[None], NEG)
    p = jax.nn.softmax(s.reshape(B, rows, H, GRID_W, kh * GRID_W), axis=-1)
    p = p.reshape(B, rows, H, GRID_W, kh, GRID_W).astype(v.dtype)
    o = jnp.einsum('brhqak,brakhe->brqhe', p, v_rows)
    return o.reshape(B, S, H, E)


def token_mixer(h, w_in, b_gate, rpb, w_branch_a, w_branch_b, w_out):
    B, S, _ = h.shape
    proj = h @ w_in
    o0 = 0
    qa = proj[..., o0:o0 + A_WIDTH].reshape(B, S, A_HEADS, HEAD_DIM); o0 += A_WIDTH
    ka = proj[..., o0:o0 + A_WIDTH].reshape(B, S, A_HEADS, HEAD_DIM); o0 += A_WIDTH
    va = proj[..., o0:o0 + A_WIDTH].reshape(B, S, A_HEADS, HEAD_DIM); o0 += A_WIDTH
    qb = proj[..., o0:o0 + B_WIDTH].reshape(B, S, B_HEADS, HEAD_DIM); o0 += B_WIDTH
    kb = proj[..., o0:o0 + B_WIDTH].reshape(B, S, B_HEADS, HEAD_DIM); o0 += B_WIDTH
    vb = proj[..., o0:o0 + B_WIDTH].reshape(B, S, B_HEADS, HEAD_DIM); o0 += B_WIDTH
    g = proj[..., o0:o0 + GATE_WIDTH] + b_gate

    qa = partial_rotary(qa)
    ka = partial_rotary(ka)
    outs, lses = [], []
    for gi, (window, dilation) in enumerate(DILATED_PAIRS):
        sl = slice(gi * A_HEADS_PER_GROUP, (gi + 1) * A_HEADS_PER_GROUP)
        o, lse = dilated_window_group(qa[:, :, sl], ka[:, :, sl], va[:, :, sl], window, dilation)
        outs.append(o)
        lses.append(lse)
    alpha = jax.nn.softmax(jnp.stack(lses, axis=0), axis=0)
    oa = jnp.sum(alpha[..., None] * jnp.stack(outs, axis=0).astype(jnp.float32), axis=0)
    ya = oa.astype(h.dtype).reshape(B, S, A_OUT) @ w_branch_a

    ob = neighbourhood_attention(qb, kb, vb, rpb)
    yb = ob.reshape(B, S, B_WIDTH) @ w_branch_b

    gates = jax.nn.sigmoid(g.astype(jnp.float32)).astype(h.dtype)
    ga, gb = gates[..., :D_MODEL], gates[..., D_MODEL:]
    return (ga * ya + gb * yb) @ w_out


def encoder_stack(x, ffn1_pre_g, ffn1_post_g, ffn1_w_gate, ffn1_w_up, ffn1_w_down,
                  mix_pre_g, mix_post_g, w_in, b_gate, rpb, w_branch_a, w_branch_b, w_out,
                  ffn2_pre_g, ffn2_post_g, ffn2_w_gate, ffn2_w_up, ffn2_w_down):
    for l in range(DEPTH):
        h = rms_norm(x, ffn1_pre_g[l])
        x = x + 0.5 * rms_norm(swiglu(h, ffn1_w_gate[l], ffn1_w_up[l], ffn1_w_down[l]), ffn1_post_g[l])
        h = rms_norm(x, mix_pre_g[l])
        m = token_mixer(h, w_in[l], b_gate[l], rpb[l], w_branch_a[l], w_branch_b[l], w_out[l])
        x = x + rms_norm(m, mix_post_g[l])
        h = rms_norm(x, ffn2_pre_g[l])
        x = x + 0.5 * rms_norm(swiglu(h, ffn2_w_gate[l], ffn2_w_up[l], ffn2_w_down[l]), ffn2_post_g[l])
    return x


def setup_inputs(seed: int = 0) -> dict:
    key = jax.random.key(seed)
    ks = jax.random.split(key, 20)
    f32 = jnp.float32

    def nrm(k, shape, scale):
        return jax.random.normal(k, shape, f32) * scale

    def gain(k):
        return 1.0 + 0.1 * jax.random.normal(k, (DEPTH, D_MODEL), f32)

    return {
        "x_prompt": jax.random.normal(ks[0], (BATCH, SEQ, D_MODEL), f32),
        "x_sample": jax.random.normal(ks[1], (DEC_BATCH, DEC_SEQ, D_MODEL), f32),
        "ffn1_pre_g": gain(ks[2]),
        "ffn1_post_g": gain(ks[3]),
        "ffn1_w_gate": nrm(ks[4], (DEPTH, D_MODEL, D_FF), D_MODEL ** -0.5),
        "ffn1_w_up": nrm(ks[5], (DEPTH, D_MODEL, D_FF), D_MODEL ** -0.5),
        "ffn1_w_down": nrm(ks[6], (DEPTH, D_FF, D_MODEL), D_FF ** -0.5),
        "mix_pre_g": gain(ks[7]),
        "mix_post_g": gain(ks[8]),
        "w_in": nrm(ks[9], (DEPTH, D_MODEL, IN_WIDTH), D_MODEL ** -0.5),
        "b_gate": nrm(ks[10], (DEPTH, GATE_WIDTH), 0.1),
        "rpb": nrm(ks[11], (DEPTH, B_HEADS, 2 * NA_ROWS - 1, 2 * NA_COLS - 1), 0.5),
        "w_branch_a": nrm(ks[12], (DEPTH, A_OUT, D_MODEL), A_OUT ** -0.5),
        "w_branch_b": nrm(ks[13], (DEPTH, B_WIDTH, D_MODEL), B_WIDTH ** -0.5),
        "w_out": nrm(ks[14], (DEPTH, D_MODEL, D_MODEL), D_MODEL ** -0.5),
        "ffn2_pre_g": gain(ks[15]),
        "ffn2_post_g": gain(ks[16]),
        "ffn2_w_gate": nrm(ks[17], (DEPTH, D_MODEL, D_FF), D_MODEL ** -0.5),
        "ffn2_w_up": nrm(ks[18], (DEPTH, D_MODEL, D_FF), D_MODEL ** -0.5),
        "ffn2_w_down": nrm(ks[19], (DEPTH, D_FF, D_MODEL), D_FF ** -0.5),
    }


def reference(x_prompt, x_sample, ffn1_pre_g, ffn1_post_g, ffn1_w_gate, ffn1_w_up, ffn1_w_down,
              mix_pre_g, mix_post_g, w_in, b_gate, rpb, w_branch_a, w_branch_b, w_out,
              ffn2_pre_g, ffn2_post_g, ffn2_w_gate, ffn2_w_up, ffn2_w_down):
    y_prompt = encoder_stack(x_prompt, ffn1_pre_g, ffn1_post_g, ffn1_w_gate, ffn1_w_up, ffn1_w_down,
                             mix_pre_g, mix_post_g, w_in, b_gate, rpb, w_branch_a, w_branch_b, w_out,
                             ffn2_pre_g, ffn2_post_g, ffn2_w_gate, ffn2_w_up, ffn2_w_down)
    y_sample = encoder_stack(x_sample, ffn1_pre_g, ffn1_post_g, ffn1_w_gate, ffn1_w_up, ffn1_w_down,
                             mix_pre_g, mix_post_g, w_in, b_gate, rpb, w_branch_a, w_branch_b, w_out,
                             ffn2_pre_g, ffn2_post_g, ffn2_w_gate, ffn2_w_up, ffn2_w_down)
    return (y_prompt, y_sample)
```

```python
from contextlib import ExitStack
import numpy as np
import ml_dtypes
import concourse.bass as bass
import concourse.mybir as mybir
from concourse.bass_utils import run_bass_kernel_spmd

F32 = mybir.dt.float32
BF16 = mybir.dt.bfloat16
ALU = mybir.AluOpType
AF = mybir.ActivationFunctionType

D = 1024
DFF = 2816
NF = DFF // 128
HALO = 1024
PIECES = (2048, 4096)
EOFF = (0, 2048 + 2 * HALO)
OOFF = (0, 2048)
TE = sum(n + 2 * HALO for n in PIECES)
NO = sum(PIECES)
ST = 512
EPS = 1e-6
NCORES = 8
INW = 5888

ENGS = ("pe", "act", "dve", "pool", "sp")
STRICT = True
NDS = 24


class Sched:
    def __init__(self, nc, es):
        self.nc = nc
        self.sem = {e: es.enter_context(nc.semaphore("s_" + e)) for e in ENGS}
        self.dsem = [es.enter_context(nc.semaphore("d%d" % i)) for i in range(NDS)]
        self.bar = es.enter_context(nc.semaphore("bar"))
        self.fin = es.enter_context(nc.semaphore("fin"))
        self.nbar = 0
        self.dcum = [0] * NDS
        self.dnext = 0
        self.cnt = {e: 0 for e in ENGS}
        self.waited = {e: {} for e in ENGS}
        self.stream = {e: [] for e in ENGS}
        self.lastw = {}
        self.readers = {}

    def _semh(self, key):
        return self.sem[key[1]] if key[0] == "e" else self.dsem[key[1]]

    def add(self, eng, fns, reads=(), writes=(), dma=False):
        if not isinstance(fns, (list, tuple)):
            fns = [fns]
        deps = set()
        for r in reads:
            if r in self.lastw:
                deps.add(self.lastw[r])
        for w in writes:
            if w in self.lastw:
                deps.add(self.lastw[w])
            for t in self.readers.get(w, ()):
                deps.add(t)
        if dma:
            k = self.dnext
            self.dnext = (k + 1) % NDS
            if self.dcum[k] > 0:
                deps.add((("d", k), self.dcum[k]))
            self.dcum[k] += 16
            token = (("d", k), self.dcum[k])
            inc = (self.dsem[k], 16)
        else:
            self.cnt[eng] += 1
            token = (("e", eng), self.cnt[eng])
            inc = (self.sem[eng], 1)
        need = {}
        for key, val in deps:
            if key == ("e", eng) and not dma and (eng == "pe" or not STRICT):
                continue
            if val > need.get(key, 0):
                need[key] = val
        for key, val in need.items():
            if self.waited[eng].get(key, 0) >= val:
                continue
            self.waited[eng][key] = val
            self.stream[eng].append(("w", self._semh(key), val))
        self.stream[eng].append(("o", list(fns), inc))
        for r in reads:
            self.readers.setdefault(r, []).append(token)
        for w in writes:
            self.lastw[w] = token
            self.readers[w] = []
        return token

    def flush(self):
        nc = self.nc
        self.nbar += 1
        target = self.nbar * len(ENGS)
        for e in ("sp", "pool", "act"):
            for k in range(NDS):
                if self.dcum[k] > self.waited[e].get(("d", k), 0):
                    self.waited[e][("d", k)] = self.dcum[k]
                    self.stream[e].append(("w", self.dsem[k], self.dcum[k]))
        streams = self.stream
        bar = self.bar

        def run(eng_obj, ops):
            for op in ops:
                if op[0] == "w":
                    eng_obj.wait_ge(op[1], op[2])
                else:
                    last = None
                    for fn in op[1]:
                        last = fn(eng_obj)
                    last.then_inc(op[2][0], op[2][1])
            eng_obj.sem_inc(bar, 1)
            eng_obj.wait_ge(bar, target)

        with nc.Block() as block:
            @block.tensor
            def _(e):
                run(e, streams["pe"])

            @block.scalar
            def _(e):
                run(e, streams["act"])

            @block.vector
            def _(e):
                run(e, streams["dve"])

            @block.gpsimd
            def _(e):
                run(e, streams["pool"])

            @block.sync
            def _(e):
                run(e, streams["sp"])
        self.stream = {e: [] for e in ENGS}
        self.lastw = {}
        self.readers = {}


def sched_finish(S, es_unused=None):
    nc = S.nc
    fin = S.fin
    allsems = list(S.sem.values()) + list(S.dsem) + [S.bar]
    with nc.Block() as block:
        @block.tensor
        def _(e):
            e.sem_inc(fin, 1)

        @block.scalar
        def _(e):
            e.sem_inc(fin, 1)

        @block.vector
        def _(e):
            e.sem_inc(fin, 1)

        @block.gpsimd
        def _(e):
            e.sem_inc(fin, 1)

        @block.sync
        def _(e):
            e.wait_ge(fin, 4)
            for sm in allsems:
                e.sem_clear(sm)
            e.sem_clear(fin)


def make_identity(S, idn, idf):
    S.add("pool", lambda e: e.iota(idf[:], [[1, 128]], channel_multiplier=-1, allow_small_or_imprecise_dtypes=True),
          writes=["idf"])
    S.add("pool", lambda e: e.tensor_single_scalar(out=idf[:], in_=idf[:], scalar=0.0, op=ALU.is_equal),
          reads=["idf"], writes=["idf"])
    S.add("pool", lambda e: e.tensor_copy(out=idn[:], in_=idf[:]), reads=["idf"], writes=["idn"])


def bcast_rows(ap_1d, nparts=128):
    return bass.AP(ap_1d.tensor, ap_1d.offset, [[0, nparts]] + [list(x) for x in ap_1d.ap])


def ffn_phase(nc, S, name, x_src, ntok, w_gate, w_up, w_down, pre_g, post_g, emit_out):
    nst = ntok // ST
    with ExitStack() as es:
        wg = es.enter_context(nc.sbuf_tensor(name + "wg", [128, 8, DFF], BF16))
        wu = es.enter_context(nc.sbuf_tensor(name + "wu", [128, 8, DFF], BF16))
        wd = es.enter_context(nc.sbuf_tensor(name + "wd", [128, NF, D], BF16))
        gpost = es.enter_context(nc.sbuf_tensor(name + "gpost", [128, D], F32))
        gpre = es.enter_context(nc.sbuf_tensor(name + "gpre", [128, 8], F32))
        epsb = es.enter_context(nc.sbuf_tensor(name + "epsb", [128, 1], F32))
        idn = es.enter_context(nc.sbuf_tensor(name + "idn", [128, 128], BF16))
        with ExitStack() as es2:
            stg = [es2.enter_context(nc.sbuf_tensor(name + "stg%d" % i, [128, DFF], F32)) for i in range(3)]
            idf = es2.enter_context(nc.sbuf_tensor(name + "idf", [128, 128], F32))
            S.add("sp", lambda e: e.dma_start(out=gpost[:], in_=bcast_rows(post_g)), writes=["gpost"], dma=True)
            S.add("sp", lambda e: e.dma_start(out=gpre[:], in_=pre_g.rearrange("(c p) -> p c", p=128),
                                              allow_slow_non_contiguous=True), writes=["gpre"], dma=True)
            S.add("pool", lambda e: e.memset(epsb[:], EPS), writes=["epsb"])
            make_identity(S, idn, idf)
            S.add("dve", lambda e: e.tensor_scalar(out=gpost[:], in0=gpost[:], scalar1=0.5, scalar2=None,
                                                   op0=ALU.mult), reads=["gpost"], writes=["gpost"])
            k = 0
            cv = ("act", "dve", "pool")
            for (wsrc, wdst, nch, width, scaled) in ((w_gate, wg, 8, DFF, True), (w_up, wu, 8, DFF, True),
                                                     (w_down, wd, NF, D, False)):
                for c in range(nch):
                    sl = k % 3
                    S.add("sp", lambda e, wsrc=wsrc, c=c, sl=sl, width=width: e.dma_start(
                        out=stg[sl][:, 0:width], in_=wsrc[c * 128:(c + 1) * 128, :]),
                        writes=[("stg", sl)], dma=True)
                    eng = "act" if scaled else "dve"
                    rd = [("stg", sl)] + (["gpre"] if scaled else [])
                    if scaled:
                        if eng == "act":
                            fn = lambda e, wdst=wdst, c=c, sl=sl, width=width: e.activation(
                                out=wdst[:, c, :], in_=stg[sl][:, 0:width], func=AF.Copy, scale=gpre[:, c:c + 1])
                        else:
                            fn = lambda e, wdst=wdst, c=c, sl=sl, width=width: e.tensor_scalar(
                                out=wdst[:, c, :], in0=stg[sl][:, 0:width], scalar1=gpre[:, c:c + 1], scalar2=None,
                                op0=ALU.mult)
                    else:
                        if eng == "act":
                            fn = lambda e, wdst=wdst, c=c, sl=sl, width=width: e.activation(
                                out=wdst[:, c, :], in_=stg[sl][:, 0:width], func=AF.Copy)
                        else:
                            fn = lambda e, wdst=wdst, c=c, sl=sl, width=width: e.tensor_copy(
                                out=wdst[:, c, :], in_=stg[sl][:, 0:width])
                    S.add(eng, fn, reads=rd, writes=[("w", id(wdst), c)])
                    k += 1
            S.flush()
        xa = [es.enter_context(nc.sbuf_tensor(name + "xa%d" % i, [128, D], F32)) for i in range(2)]
        xr = [es.enter_context(nc.sbuf_tensor(name + "xr%d" % i, [128, D], F32)) for i in range(2)]
        tt = [es.enter_context(nc.sbuf_tensor(name + "tt%d" % i, [128, D], F32)) for i in range(2)]
        hb = [es.enter_context(nc.sbuf_tensor(name + "hb%d" % i, [128, D], BF16)) for i in range(4)]
        hT = es.enter_context(nc.sbuf_tensor(name + "hT", [128, 8, ST], BF16))
        aT = es.enter_context(nc.sbuf_tensor(name + "aT", [128, NF, ST], BF16))
        sg = [es.enter_context(nc.sbuf_tensor(name + "sg%d" % i, [128, ST], F32)) for i in range(2)]
        ob = [es.enter_context(nc.sbuf_tensor(name + "ob%d" % i, [128, D], BF16)) for i in range(2)]
        st_a = es.enter_context(nc.sbuf_tensor(name + "sta", [128, 8], F32))
        st_y = es.enter_context(nc.sbuf_tensor(name + "sty", [128, 32], F32))
        st_o = es.enter_context(nc.sbuf_tensor(name + "sto", [128, 8], F32))
        pt = [es.enter_context(nc.psum_tensor(name + "pt%d" % i, [128, 8, 128], BF16)) for i in range(2)]
        pg = [es.enter_context(nc.psum_tensor(name + "pg%d" % i, [128, ST], F32)) for i in range(2)]
        pu = [es.enter_context(nc.psum_tensor(name + "pu%d" % i, [128, ST], F32)) for i in range(2)]
        py = es.enter_context(nc.psum_tensor(name + "py", [128, D], F32))
        bufs = dict(ob=ob, st_o=st_o, hb=hb, epsb=epsb)

        def stage_a_pre(s):
            for j in range(4):
                jj = s * 4 + j
                sl = jj % 2
                S.add("sp", lambda e, jj=jj, sl=sl: e.dma_start(out=xa[sl][:], in_=x_src[jj * 128:(jj + 1) * 128, :]),
                      writes=[("xa", sl)], dma=True)
                c0 = (jj % 4) * 2
                S.add("act", lambda e, sl=sl, c0=c0, j=j: e.activation(out=hb[j][:], in_=xa[sl][:], func=AF.Square,
                                                                    accum_out=st_a[:, c0:c0 + 1]),
                      reads=[("xa", sl)], writes=[("hb", j), ("sta", c0)])
                S.add("act", lambda e, c0=c0: e.activation(out=st_a[:, c0 + 1:c0 + 2], in_=st_a[:, c0:c0 + 1],
                                                         func=AF.Sqrt, scale=1.0 / D, bias=epsb[:, 0:1]),
                      reads=[("sta", c0)], writes=[("sta", c0 + 1)])
                S.add("dve", lambda e, c0=c0: e.reciprocal(out=st_a[:, c0 + 1:c0 + 2], in_=st_a[:, c0 + 1:c0 + 2]),
                      reads=[("sta", c0 + 1)], writes=[("sta", c0 + 1)])
                S.add("dve", lambda e, sl=sl, c0=c0, j=j: e.tensor_scalar(out=hb[j][:], in0=xa[sl][:],
                                                                         scalar1=st_a[:, c0 + 1:c0 + 2], scalar2=None,
                                                                         op0=ALU.mult),
                      reads=[("xa", sl), ("sta", c0 + 1)], writes=[("hb", j)])

        def stage_a_T(s):
            for j in range(4):
                sl = j % 2
                S.add("pe", [lambda e, sl=sl, c=c, j=j: e.transpose(pt[sl][:, c, :], hb[j][:, c * 128:(c + 1) * 128], idn[:])
                             for c in range(8)], reads=[("hb", j)], writes=[("pt", sl)])
                S.add("act", lambda e, sl=sl, j=j: e.activation(out=hT[:, :, j * 128:(j + 1) * 128], in_=pt[sl][:],
                                                               func=AF.Copy),
                      reads=[("pt", sl)], writes=[("hT", j)])

        def stage_gu(s, f0, f1):
            for f in range(f0, f1):
                sl = f % 2
                S.add("pe", [lambda e, c=c, f=f, sl=sl: e.matmul(pg[sl][:], lhsT=wg[:, c, f * 128:(f + 1) * 128],
                                                                rhs=hT[:, c, :], start=(c == 0), stop=(c == 7))
                             for c in range(8)], reads=[("hT", j) for j in range(4)], writes=[("pg", sl)])
                S.add("pe", [lambda e, c=c, f=f, sl=sl: e.matmul(pu[sl][:], lhsT=wu[:, c, f * 128:(f + 1) * 128],
                                                                rhs=hT[:, c, :], start=(c == 0), stop=(c == 7))
                             for c in range(8)], reads=[("hT", j) for j in range(4)], writes=[("pu", sl)])
                S.add("act", lambda e, sl=sl: e.activation(out=sg[sl][:], in_=pg[sl][:], func=AF.Silu),
                      reads=[("pg", sl)], writes=[("sg", sl)])
                S.add("dve", lambda e, sl=sl, f=f: e.tensor_tensor(out=aT[:, f, :], in0=sg[sl][:], in1=pu[sl][:],
                                                                  op=ALU.mult),
                      reads=[("sg", sl), ("pu", sl)], writes=[("aT", f)])

        def down_chain(jj, j):
            sl = jj % 2
            ts = jj % 2
            c0 = (jj % 4) * 8
            S.add("act", lambda e, c0=c0, ts=ts, j=j: e.activation(out=hb[j][:], in_=tt[ts][:], func=AF.Square,
                                                               accum_out=st_y[:, c0 + 4:c0 + 5]),
                  reads=[("tt", ts, 0), ("tt", ts, 1)], writes=[("hb", j), ("sty", c0 + 4)])
            S.add("act", lambda e, c0=c0: e.activation(out=st_y[:, c0 + 6:c0 + 7], in_=st_y[:, c0 + 4:c0 + 5],
                                                     func=AF.Sqrt, scale=1.0 / D, bias=epsb[:, 0:1]),
                  reads=[("sty", c0 + 4)], writes=[("sty", c0 + 6)])
            S.add("dve", lambda e, c0=c0: e.reciprocal(out=st_y[:, c0 + 6:c0 + 7], in_=st_y[:, c0 + 6:c0 + 7]),
                  reads=[("sty", c0 + 6)], writes=[("sty", c0 + 6)])
            S.add("dve", lambda e, c0=c0, ts=ts: e.scalar_tensor_tensor(out=tt[ts][:], in0=tt[ts][:], scalar=st_y[:, c0 + 6:c0 + 7],
                                                                      in1=gpost[:], op0=ALU.mult, op1=ALU.mult),
                  reads=[("tt", ts, 0), ("tt", ts, 1), ("sty", c0 + 6)], writes=[("tt", ts, 0), ("tt", ts, 1)])
            S.add("dve", lambda e, sl=sl, ts=ts: e.tensor_tensor(out=xr[sl][:], in0=tt[ts][:], in1=xr[sl][:], op=ALU.add),
                  reads=[("tt", ts, 0), ("tt", ts, 1), ("xr", sl)], writes=[("xr", sl)])
            emit_out(S, jj, xr[sl], ("xr", sl), bufs)

        def stage_down(s):
            for j in range(4):
                jj = s * 4 + j
                sl = jj % 2
                ts = jj % 2
                S.add("sp", lambda e, jj=jj, sl=sl: e.dma_start(out=xr[sl][:], in_=x_src[jj * 128:(jj + 1) * 128, :]),
                      writes=[("xr", sl)], dma=True)
                for half in range(2):
                    S.add("pe", [lambda e, f=f, j=j, half=half: e.matmul(
                        py[:, half * 512:(half + 1) * 512], lhsT=aT[:, f, j * 128:(j + 1) * 128],
                        rhs=wd[:, f, half * 512:(half + 1) * 512], start=(f == 0), stop=(f == NF - 1))
                        for f in range(NF)], reads=[("aT", f) for f in range(NF)], writes=[("py", half)])
                    S.add("dve", lambda e, half=half, ts=ts: e.tensor_copy(out=tt[ts][:, half * 512:(half + 1) * 512],
                                                                           in_=py[:, half * 512:(half + 1) * 512]),
                          reads=[("py", half)], writes=[("tt", ts, half)])
                if j > 0:
                    down_chain(jj - 1, j - 1)
            down_chain(s * 4 + 3, 3)

        stage_a_pre(0)
        stage_a_T(0)
        for s in range(nst):
            stage_gu(s, 0, 10)
            if s + 1 < nst:
                stage_a_pre(s + 1)
            stage_gu(s, 10, NF)
            if s + 1 < nst:
                stage_a_T(s + 1)
            stage_down(s)
        S.flush()


MT = 2048
HK_A, HQ_A, HK_B, HQ_B = 0, 12, 24, 32
COL = dict(QA=0, KA=768, VA=1536, QB=2304, KB=2816, VB=3328, G=3840)


def own_block(t0):
    for p, n in enumerate(PIECES):
        lo = EOFF[p] + HALO
        if lo <= t0 < lo + n:
            return OOFF[p] + (t0 - lo)
    return None


def p2_phase(nc, S, h2, w_in, mixpre, b_gate, rope, qkT, vtA, vtB, gT, nmt):
    with ExitStack() as es:
        win = es.enter_context(nc.sbuf_tensor("win", [128, 8, INW], BF16))
        wr = es.enter_context(nc.sbuf_tensor("wr", [128, 8, 384], BF16))
        wrp = es.enter_context(nc.sbuf_tensor("wrp", [128, 8, 384], BF16))
        gmix = es.enter_context(nc.sbuf_tensor("gmix", [128, 8], F32))
        bg = es.enter_context(nc.sbuf_tensor("bg", [128, 16], F32))
        idn = es.enter_context(nc.sbuf_tensor("p2idn", [128, 128], BF16))
        with ExitStack() as es2:
            stg = [es2.enter_context(nc.sbuf_tensor("p2stg%d" % i, [128, INW], F32)) for i in range(2)]
            idf = es2.enter_context(nc.sbuf_tensor("p2idf", [128, 128], F32))
            S.add("sp", lambda e: e.dma_start(out=gmix[:], in_=mixpre.rearrange("(c p) -> p c", p=128),
                                              allow_slow_non_contiguous=True), writes=["gmix"], dma=True)
            S.add("sp", lambda e: e.dma_start(out=bg[:], in_=b_gate.rearrange("(c p) -> p c", p=128),
                                              allow_slow_non_contiguous=True), writes=["bg"], dma=True)
            make_identity(S, idn, idf)
            for c in range(8):
                sl = c % 2
                S.add("sp", lambda e, c=c, sl=sl: e.dma_start(out=stg[sl][:], in_=w_in[c * 128:(c + 1) * 128, :]),
                      writes=[("stg", sl)], dma=True)
                sc = gmix[:, c:c + 1]
                S.add("act", lambda e, c=c, sl=sl, sc=sc: e.activation(out=win[:, c, 0:3072], in_=stg[sl][:, 0:3072],
                                                                      func=AF.Copy, scale=sc),
                      reads=[("stg", sl), "gmix"], writes=[("win", c, 0)])
                S.add("act", lambda e, c=c, sl=sl, sc=sc: e.activation(out=win[:, c, 3072:INW], in_=stg[sl][:, 3072:INW],
                                                                      func=AF.Copy, scale=sc),
                      reads=[("stg", sl), "gmix"], writes=[("win", c, 1)])
                fns = []
                for (dst0, src0) in ((0, COL["KA"]), (192, COL["QA"])):
                    def mk(dst_t, dlo, slo, dst0=dst0, src0=src0, c=c, sl=sl, sc=sc):
                        def fn(e):
                            src = stg[sl][:, src0:src0 + 768].rearrange("p (h e) -> p h e", e=64)[:, :, slo:slo + (16 if dst_t is wr else 8)]
                            n = 16 if dst_t is wr else 8
                            dst = dst_t[:, c, dst0:dst0 + 192].rearrange("p (h e) -> p h e", e=16)[:, :, dlo:dlo + n]
                            return e.activation(out=dst, in_=src, func=AF.Copy, scale=sc)
                        return fn
                    fns.append(mk(wr, 0, 0))
                    fns.append(mk(wrp, 0, 8))
                    fns.append(mk(wrp, 8, 0))
                S.add("act", fns, reads=[("stg", sl), "gmix"], writes=[("wr", c)])
            S.flush()
        hb = [es.enter_context(nc.sbuf_tensor("p2hb%d" % i, [128, D], BF16)) for i in range(2)]
        hT = es.enter_context(nc.sbuf_tensor("p2hT", [128, 8, MT], BF16))
        fo = [es.enter_context(nc.sbuf_tensor("p2fo%d" % i, [128, 512], BF16)) for i in range(4)]
        cs = [es.enter_context(nc.sbuf_tensor("p2cs%d" % i, [128, 2, 512], F32)) for i in range(2)]
        r1 = es.enter_context(nc.sbuf_tensor("p2r1", [128, 512], F32))
        r2 = es.enter_context(nc.sbuf_tensor("p2r2", [128, 512], F32))
        ro = [es.enter_context(nc.sbuf_tensor("p2ro%d" % i, [128, 512], BF16)) for i in range(2)]
        vo = [es.enter_context(nc.sbuf_tensor("p2vo%d" % i, [128, 512], BF16)) for i in range(2)]
        pt = [es.enter_context(nc.psum_tensor("p2pt%d" % i, [128, 8, 128], BF16)) for i in range(2)]
        pf = [es.enter_context(nc.psum_tensor("p2pf%d" % i, [128, 512], F32)) for i in range(4)]
        pv = [es.enter_context(nc.psum_tensor("p2pv%d" % i, [128, 512], F32)) for i in range(2)]
        cnt = dict(f=0, v=0, fo=0, ro=0, cs=0)
        qk2 = qkT

        def fm_tile(cols0, hTsl):
            k = cnt["f"] % 4
            cnt["f"] += 1
            S.add("pe", [lambda e, c=c, k=k: e.matmul(pf[k][:], lhsT=cols0(c), rhs=hTsl(c), start=(c == 0), stop=(c == 7))
                         for c in range(8)], reads=["hT"], writes=[("pf", k)])
            return k

        for mt in range(nmt):
            T0 = mt * MT
            for j in range(MT // 128):
                sl = j % 2
                t0 = T0 + j * 128
                S.add("sp", lambda e, t0=t0, sl=sl: e.dma_start(out=hb[sl][:], in_=h2[t0:t0 + 128, :]),
                      writes=[("hb", sl)], dma=True)
                S.add("pe", [lambda e, sl=sl, c=c: e.transpose(pt[sl][:, c, :], hb[sl][:, c * 128:(c + 1) * 128], idn[:])
                             for c in range(8)], reads=[("hb", sl)], writes=[("pt", sl)])
                eng = "act" if j % 2 == 0 else "dve"
                if eng == "act":
                    fn = lambda e, sl=sl, j=j: e.activation(out=hT[:, :, j * 128:(j + 1) * 128], in_=pt[sl][:], func=AF.Copy)
                else:
                    fn = lambda e, sl=sl, j=j: e.tensor_copy(out=hT[:, :, j * 128:(j + 1) * 128], in_=pt[sl][:])
                S.add(eng, fn, reads=[("pt", sl)], writes=["hT"])
            for b in range(MT // ST):
                tb = T0 + b * ST
                own = own_block(tb)
                hsl = lambda c, b=b: hT[:, c, b * ST:(b + 1) * ST]
                jobs = [(COL["KA"] + 128 * i, (HK_A * 64) + 128 * i) for i in range(6)]
                jobs += [(COL["KB"] + 128 * i, (HK_B * 64) + 128 * i) for i in range(4)]
                if own is not None:
                    jobs += [(COL["QA"] + 128 * i, (HQ_A * 64) + 128 * i) for i in range(6)]
                    jobs += [(COL["QB"] + 128 * i, (HQ_B * 64) + 128 * i) for i in range(4)]
                for (wc, row0) in jobs:
                    k = fm_tile(lambda c, wc=wc: win[:, c, wc:wc + 128], hsl)
                    o = cnt["fo"] % 4
                    cnt["fo"] += 1
                    eng = "act" if cnt["fo"] % 2 == 0 else "dve"
                    if eng == "act":
                        fn = lambda e, o=o, k=k: e.activation(out=fo[o][:], in_=pf[k][:], func=AF.Copy)
                    else:
                        fn = lambda e, o=o, k=k: e.tensor_copy(out=fo[o][:], in_=pf[k][:])
                    S.add(eng, fn, reads=[("pf", k)], writes=[("fo", o)])
                    S.add("sp", lambda e, o=o, row0=row0, tb=tb: e.dma_start(out=qk2[row0:row0 + 128, tb:tb + ST], in_=fo[o][:]),
                          reads=[("fo", o)], writes=[("qk", row0, tb)], dma=True)
                if own is not None:
                    for i in range(16):
                        wc = COL["G"] + 128 * i
                        k = fm_tile(lambda c, wc=wc: win[:, c, wc:wc + 128], hsl)
                        o = cnt["fo"] % 4
                        cnt["fo"] += 1
                        S.add("act", lambda e, o=o, k=k, i=i: e.activation(out=fo[o][:], in_=pf[k][:], func=AF.Sigmoid,
                                                                        bias=bg[:, i:i + 1]),
                              reads=[("pf", k), "bg"], writes=[("fo", o)])
                        S.add("sp", lambda e, o=o, i=i, own=own: e.dma_start(out=gT[i * 128:(i + 1) * 128, own:own + ST], in_=fo[o][:]),
                              reads=[("fo", o)], writes=[("gT", i, own)], dma=True)
                csl = cnt["cs"] % 2
                cnt["cs"] += 1
                S.add("sp", lambda e, csl=csl, tb=tb: e.dma_start(out=cs[csl][:], in_=rope[:, :, tb:tb + ST].rearrange("a p t -> p a t")),
                      writes=[("cs", csl)], dma=True)
                for rc in range(3 if own is not None else 2):
                    k1 = fm_tile(lambda c, rc=rc: wr[:, c, rc * 128:(rc + 1) * 128], hsl)
                    k2 = fm_tile(lambda c, rc=rc: wrp[:, c, rc * 128:(rc + 1) * 128], hsl)
                    o = cnt["ro"] % 2
                    cnt["ro"] += 1
                    S.add("dve", lambda e, k1=k1, csl=csl: e.tensor_tensor(out=r1[:], in0=pf[k1][:], in1=cs[csl][:, 0, :], op=ALU.mult),
                          reads=[("pf", k1), ("cs", csl)], writes=["r1"])
                    S.add("dve", lambda e, k2=k2, csl=csl: e.tensor_tensor(out=r2[:], in0=pf[k2][:], in1=cs[csl][:, 1, :], op=ALU.mult),
                          reads=[("pf", k2), ("cs", csl)], writes=["r2"])
                    S.add("dve", lambda e, o=o: e.tensor_tensor(out=ro[o][:], in0=r1[:], in1=r2[:], op=ALU.add),
                          reads=["r1", "r2"], writes=[("ro", o)])
                    for h8 in range(8):
                        rh = rc * 8 + h8
                        if rh >= 12 and own is None:
                            continue
                        head = (HK_A + rh) if rh < 12 else (HQ_A + rh - 12)
                        wkeys = [("qk", (head // 2) * 128, tb)]
                        S.add("sp", lambda e, o=o, h8=h8, head=head, tb=tb: e.dma_start(
                            out=qk2[head * 64:head * 64 + 16, tb:tb + ST], in_=ro[o][h8 * 16:(h8 + 1) * 16, :]),
                            reads=[("ro", o)], writes=wkeys, dma=True)
            for g, d in enumerate((1, 4, 16)):
                for lt in range(16):
                    if d == 1:
                        sel = lambda c, lt=lt: hT[:, c, lt * 128:(lt + 1) * 128]
                    elif d == 4:
                        s4, r = lt // 4, lt % 4
                        sel = lambda c, s4=s4, r=r: hT[:, c, s4 * 512 + r:s4 * 512 + 512:4]
                    else:
                        sel = lambda c, lt=lt: hT[:, c, lt:MT:16]
                    k = cnt["v"] % 2
                    cnt["v"] += 1
                    wc = COL["VA"] + 256 * g
                    S.add("pe", [lambda e, c=c, k=k, sel=sel, wc=wc: e.matmul(pv[k][:, 0:256], lhsT=sel(c), rhs=win[:, c, wc:wc + 256],
                                                                            start=(c == 0), stop=(c == 7)) for c in range(8)],
                          reads=["hT"], writes=[("pv", k)])
                    if lt % 2 == 0:
                        S.add("act", lambda e, k=k: e.activation(out=vo[k][:, 0:256], in_=pv[k][:, 0:256], func=AF.Copy),
                              reads=[("pv", k)], writes=[("vo", k)])
                    else:
                        S.add("dve", lambda e, k=k: e.tensor_copy(out=vo[k][:, 0:256], in_=pv[k][:, 0:256]),
                              reads=[("pv", k)], writes=[("vo", k)])
                    tid = mt * 16 + lt
                    S.add("sp", lambda e, k=k, g=g, tid=tid: e.dma_start(out=vtA[g, tid, :, :], in_=vo[k][:, 0:256]),
                          reads=[("vo", k)], writes=[("vtA", g, tid)], dma=True)
            for lt in range(16):
                k = cnt["v"] % 2
                cnt["v"] += 1
                wc = COL["VB"]
                S.add("pe", [lambda e, c=c, k=k, lt=lt, wc=wc: e.matmul(pv[k][:], lhsT=hT[:, c, lt * 128:(lt + 1) * 128],
                                                                       rhs=win[:, c, wc:wc + 512], start=(c == 0), stop=(c == 7))
                             for c in range(8)], reads=["hT"], writes=[("pv", k)])
                S.add("dve", lambda e, k=k: e.tensor_copy(out=vo[k][:], in_=pv[k][:]), reads=[("pv", k)], writes=[("vo", k)])
                tid = mt * 16 + lt
                S.add("sp", lambda e, k=k, tid=tid: e.dma_start(out=vtB[tid, :, :], in_=vo[k][:]),
                      reads=[("vo", k)], writes=[("vtB", tid)], dma=True)
        S.flush()


def p3_phase(nc, S, qkT, vtA, vexp, bmA, oT):
    NB = 2
    with ExitStack() as es:
        EXTM = max(PIECES) + 2 * HALO
        KT = [es.enter_context(nc.sbuf_tensor("aKT%d" % i, [64, EXTM], BF16)) for i in range(2)]
        QT = [es.enter_context(nc.sbuf_tensor("aQT%d" % i, [64, EXTM], BF16)) for i in range(2)]
        Vt = [es.enter_context(nc.sbuf_tensor("aVt%d" % i, [128, EXTM // 128, 128], BF16)) for i in range(2)]
        acc = es.enter_context(nc.sbuf_tensor("aacc", [128, EXTM], F32))
        bm = es.enter_context(nc.sbuf_tensor("abm", [128, 2 * NB, 128], BF16))
        NSL = 4
        pT = [es.enter_context(nc.sbuf_tensor("apT%d" % i, [128, 2 * NB, 128], BF16)) for i in range(NSL)]
        FC = 1024
        rz = es.enter_context(nc.sbuf_tensor("arz", [64, FC], F32))
        oo = [es.enter_context(nc.sbuf_tensor("aoo%d" % i, [64, FC], BF16)) for i in range(2)]
        Vall = [es.enter_context(nc.sbuf_tensor("aVall%d" % i, [128, EXTM // 128, 256], BF16)) for i in range(3)]
        vx = [es.enter_context(nc.sbuf_tensor("avx%d" % i, [128, EXTM // 128, 64], BF16)) for i in range(3)]
        PS = es.enter_context(nc.psum_tensor("aPS", [128, 32, 128], F32))
        for i in range(2):
            S.add("dve", lambda e, i=i: e.memset(QT[i][:], 0.0), writes=[("QT", i)])
        for b in range(NB):
            S.add("sp", lambda e, b=b: e.dma_start(out=bm[:, 2 * b:2 * b + 2, :], in_=bmA.rearrange("p (t q) -> p t q", t=2)), writes=["bm"], dma=True)
        jn = 0
        ld = 0
        oc = 0
        pend = []
        LAG = 2
        its3 = [(p, n, hg, g, d) for p, n in enumerate(PIECES) for hg in range(4) for g, d in enumerate((1, 4, 16))]

        def loads3(it):
            p, n, hg, g, d = its3[it]
            ext = n + 2 * HALO
            nt = ext // 128
            t00 = EOFF[p] // 128
            sl = it % 2
            if hg == 0 and g == 0:
                for gg in range(3):
                    for a in range(0, nt, 8):
                        S.add("sp", lambda e, gg=gg, a=a, t00=t00: e.dma_start(
                            out=Vall[gg][:, a:a + 8, :], in_=vtA[gg, t00 + a:t00 + a + 8, :, :].rearrange("t p c -> p t c")),
                            reads=[("vt",)], writes=[("Vall", gg)], dma=True)
                    S.add("sp", lambda e, gg=gg, t00=t00, nt=nt: e.dma_start(out=vx[gg][:, 0:nt, :], in_=vexp[:, gg, t00:t00 + nt, :]),
                          writes=[("vx", gg)], dma=True)
            hk, hq = HK_A + 4 * g + hg, HQ_A + 4 * g + hg
            S.add("sp", lambda e, sl=sl, hk=hk, p=p, ext=ext: e.dma_start(
                out=KT[sl][:, 0:ext], in_=qkT[hk * 64:hk * 64 + 64, EOFF[p]:EOFF[p] + ext]),
                reads=[("qk",)], writes=[("KT", sl)], dma=True)
            S.add("sp", lambda e, sl=sl, hq=hq, p=p, n=n: e.dma_start(
                out=QT[sl][:, HALO:HALO + n], in_=qkT[hq * 64:hq * 64 + 64, EOFF[p] + HALO:EOFF[p] + HALO + n]),
                reads=[("qk",)], writes=[("QT", sl)], dma=True)
            S.add("pool", lambda e, sl=sl, g=g, hg=hg, nt=nt: e.tensor_copy(
                out=Vt[sl][:, 0:nt, 0:64], in_=Vall[g][:, 0:nt, hg * 64:(hg + 1) * 64]),
                reads=[("Vall", g)], writes=[("Vt", sl, 0)])
            S.add("act", lambda e, sl=sl, g=g, nt=nt: e.activation(
                out=Vt[sl][:, 0:nt, 64:128], in_=vx[g][:, 0:nt, :], func=AF.Copy),
                reads=[("vx", g)], writes=[("Vt", sl, 1)])

        loads3(0)
        for it, (p, n, hg, g, d) in enumerate(its3):
            if True:
                if True:
                    ext = n + 2 * HALO
                    nt = ext // 128
                    t00 = EOFF[p] // 128
                    sl = it % 2
                    if g == 0:
                        S.add("dve", lambda e: e.memset(acc[:], 0.0), writes=[("acc", b) for b in range(EXTM // 1024)])
                    i_lo = (HALO // d - 64) // 128
                    i_hi = -(-((HALO + n) // d - 64) // 128) - 1
                    tiles = [(r, i) for r in range(d) for i in range(i_lo, i_hi + 1)]
                    for b0 in range(0, len(tiles), NB):
                        if b0 == NB * (LAG + 1) and it + 1 < len(its3):
                            loads3(it + 1)
                        bt = tiles[b0:b0 + NB]
                        nb = len(bt)
                        k = jn % NSL
                        jn += 1
                        psk = lambda bi, t, k=k: PS[:, 8 * k + 2 * bi + t, :]
                        pok = lambda bi, k=k: PS[:, 8 * k + 4 + bi, :]
                        fns = []
                        for bi, (r, i) in enumerate(bt):
                            q0 = d * (128 * i + 64) + r
                            for t in range(2):
                                k0 = d * 128 * (i + t) + r
                                fns.append(lambda e, bi=bi, t=t, k0=k0, q0=q0, sl=sl, d=d, psk=psk: e.matmul(
                                    psk(bi, t), lhsT=KT[sl][:, k0:k0 + 127 * d + 1:d], rhs=QT[sl][:, q0:q0 + 127 * d + 1:d],
                                    start=True, stop=True))
                        S.add("pe", fns, reads=[("KT", sl), ("QT", sl)], writes=[("ps", k)])
                        S.add("act", lambda e, k=k, nb=nb: e.activation(
                            out=pT[k][:, 0:2 * nb, :], in_=PS[:, 8 * k:8 * k + 2 * nb, :], func=AF.Exp, scale=0.125),
                            reads=[("ps", k)], writes=[("pT", k)])
                        S.add("dve", lambda e, k=k, nb=nb: e.tensor_tensor(out=pT[k][:, 0:2 * nb, :], in0=pT[k][:, 0:2 * nb, :],
                                                                          in1=bm[:, 0:2 * nb, :], op=ALU.mult),
                              reads=[("pT", k), "bm"], writes=[("pT", k)])
                        def pv(bt=bt, nb=nb, k=k, sl=sl, d=d, g=g, pok=pok):
                            fns = []
                            for bi, (r, i) in enumerate(bt):
                                for t in range(2):
                                    fns.append(lambda e, bi=bi, t=t, k=k, sl=sl, i=i, d=d, r=r, pok=pok: e.matmul(
                                        pok(bi), lhsT=Vt[sl][:, (i + t) * d + r, :], rhs=pT[k][:, 2 * bi + t, :],
                                        start=(t == 0), stop=(t == 1)))
                            S.add("pe", fns, reads=[("pT", k), ("Vt", sl, 0), ("Vt", sl, 1)], writes=[("po", k)])
                            merged = (nb == 2 and bt[0][0] == bt[1][0] and bt[1][1] == bt[0][1] + 1)
                            if merged:
                                r, i = bt[0]
                                q0 = d * (128 * i + 64) + r
                                evs = [(acc[:, q0:q0 + 255 * d + 1:d], PS[:, 8 * k + 4:8 * k + 6, :].rearrange("p a b -> p (a b)"),
                                        [("acc", b) for b in range(q0 // 1024, (q0 + 255 * d) // 1024 + 1)])]
                            else:
                                evs = []
                                for bi, (r, i) in enumerate(bt):
                                    q0 = d * (128 * i + 64) + r
                                    evs.append((acc[:, q0:q0 + 127 * d + 1:d], pok(bi),
                                                [("acc", b) for b in range(q0 // 1024, (q0 + 127 * d) // 1024 + 1)]))
                            for (asl, src, akeys) in evs:
                                if g == 0:
                                    S.add("dve", lambda e, asl=asl, src=src: e.tensor_copy(out=asl, in_=src),
                                          reads=[("po", k)], writes=akeys)
                                else:
                                    S.add("dve", lambda e, asl=asl, src=src: e.tensor_tensor(out=asl, in0=src, in1=asl, op=ALU.add),
                                          reads=[("po", k)] + akeys, writes=akeys)

                        pend.append(pv)
                        if len(pend) > LAG:
                            pend.pop(0)()
                if g == 2:
                    while pend:
                        pend.pop(0)()
                    for c0 in range(0, n, FC):
                        o = oc % 2
                        oc += 1
                        S.add("act", lambda e, c0=c0: e.activation(out=rz[:], in_=acc[64:128, HALO + c0:HALO + c0 + FC], func=AF.Ln),
                              reads=[("acc", (HALO + c0) // 1024)], writes=["rz"])
                        S.add("act", lambda e: e.activation(out=rz[:], in_=rz[:], func=AF.Exp, scale=-1.0),
                              reads=["rz"], writes=["rz"])
                        S.add("dve", lambda e, c0=c0, o=o: e.tensor_tensor(out=oo[o][:], in0=acc[0:64, HALO + c0:HALO + c0 + FC],
                                                                          in1=rz[:], op=ALU.mult),
                              reads=[("acc", (HALO + c0) // 1024), "rz"], writes=[("oo", o)])
                        S.add("sp", lambda e, o=o, hg=hg, p=p, c0=c0: e.dma_start(
                            out=oT[hg * 64:(hg + 1) * 64, OOFF[p] + c0:OOFF[p] + c0 + FC], in_=oo[o][:]),
                            reads=[("oo", o)], writes=[("oT", hg, p, c0)], dma=True)
        S.flush()


JT_TILES = ((2, 5), (2, 6), (2, 5), (2, 5), (1, 6))
JT_OFF = (0, 5, 11, 16, 21)


def p4_phase(nc, S, qkT, vtB, bias4, mB, oT):
    with ExitStack() as es:
        EXTM = max(PIECES) + 2 * HALO
        KT = [es.enter_context(nc.sbuf_tensor("bKT%d" % i, [64, EXTM], BF16)) for i in range(2)]
        QT = [es.enter_context(nc.sbuf_tensor("bQT%d" % i, [64, EXTM], BF16)) for i in range(2)]
        Vt = [es.enter_context(nc.sbuf_tensor("bVt%d" % i, [128, EXTM // 128, 128], BF16)) for i in range(2)]
        VallB = es.enter_context(nc.sbuf_tensor("bVall", [128, EXTM // 128, 512], BF16))
        b4 = es.enter_context(nc.sbuf_tensor("bb4", [128, 8, 128], F32))
        eb = es.enter_context(nc.sbuf_tensor("beb", [128, 8, 128], F32))
        mb = es.enter_context(nc.sbuf_tensor("bmb", [128, 27, 128], BF16))
        W = [es.enter_context(nc.sbuf_tensor("bW%d" % i, [128, 27, 128], BF16)) for i in range(2)]
        NSL = 4
        pT = [es.enter_context(nc.sbuf_tensor("bpT%d" % i, [128, 6, 128], BF16)) for i in range(NSL)]
        rz = [es.enter_context(nc.sbuf_tensor("brz%d" % i, [64, 512], F32)) for i in range(2)]
        oo = [es.enter_context(nc.sbuf_tensor("boo%d" % i, [64, EXTM - 2 * HALO], BF16)) for i in range(2)]
        PS = es.enter_context(nc.psum_tensor("bPS", [128, 32, 128], F32))
        S.add("sp", lambda e: e.dma_start(out=mb[:], in_=mB), writes=["mb"], dma=True)
        for i in range(2):
            S.add("pool", lambda e, i=i: e.memset(Vt[i][:, :, 64:128], 1.0), writes=[("Vt", i)])
        jn = 0
        gn = 0
        ld = 0
        pend = []
        LAG = 2
        its = [(p, n, h) for p, n in enumerate(PIECES) for h in range(8)]

        def emit_loads(it):
            p, n, h = its[it]
            ext = n + 2 * HALO
            nt = ext // 128
            t00 = EOFF[p] // 128
            sl = it % 2
            wsl = h % 2
            if h == 0:
                for a in range(0, nt, 8):
                    S.add("sp", lambda e, a=a, t00=t00: e.dma_start(
                        out=VallB[:, a:a + 8, :], in_=vtB[t00 + a:t00 + a + 8, :, :].rearrange("t p c -> p t c")),
                        reads=[("vt",)], writes=["VallB"], dma=True)
            S.add("sp", lambda e, h=h: e.dma_start(out=b4[:], in_=bias4[h].rearrange("b p q -> p b q")), writes=["b4"], dma=True)
            hk, hq = HK_B + h, HQ_B + h
            S.add("sp", lambda e, sl=sl, hk=hk, p=p, ext=ext: e.dma_start(
                out=KT[sl][:, 0:ext], in_=qkT[hk * 64:hk * 64 + 64, EOFF[p]:EOFF[p] + ext]),
                reads=[("qk",)], writes=[("KT", sl)], dma=True)
            S.add("sp", lambda e, sl=sl, hq=hq, p=p, ext=ext: e.dma_start(
                out=QT[sl][:, 0:ext], in_=qkT[hq * 64:hq * 64 + 64, EOFF[p]:EOFF[p] + ext]),
                reads=[("qk",)], writes=[("QT", sl)], dma=True)
            S.add("pool", lambda e, sl=sl, h=h, nt=nt: e.tensor_copy(
                out=Vt[sl][:, 0:nt, 0:64], in_=VallB[:, 0:nt, h * 64:(h + 1) * 64]),
                reads=["VallB"], writes=[("Vt", sl)])

        def emit_tables(it):
            p, n, h = its[it]
            wsl = h % 2
            S.add("act", lambda e: e.activation(out=eb[:], in_=b4[:], func=AF.Exp), reads=["b4"], writes=["eb"])
            for jt in range(5):
                b0, ntl = JT_TILES[jt]
                S.add("dve", lambda e, jt=jt, b0=b0, ntl=ntl, wsl=wsl: e.tensor_tensor(
                    out=W[wsl][:, JT_OFF[jt]:JT_OFF[jt] + ntl, :], in0=eb[:, b0:b0 + ntl, :],
                    in1=mb[:, JT_OFF[jt]:JT_OFF[jt] + ntl, :], op=ALU.mult),
                    reads=["eb", "mb"], writes=[("W", wsl)])

        emit_loads(0)
        emit_tables(0)
        for it, (p, n, h) in enumerate(its):
            if it + 1 < len(its):
                emit_loads(it + 1)
            if True:
                wsl = h % 2
                sl = it % 2
                nrp = n // 128
                osl = h % 2
                for rp in range(nrp):
                    jt = 1 if rp == 0 else 2 if rp == 1 else 4 if rp == nrp - 1 else 3 if rp == nrp - 2 else 0
                    b0, ntl = JT_TILES[jt]
                    dlt0 = -8 + 2 * b0
                    if rp == 8 and it + 1 < len(its):
                        emit_tables(it + 1)
                    k = jn % NSL
                    jn += 1
                    gk = k
                    gi = 0
                    poq = 8 * k + 6
                    q0 = HALO + 128 * rp
                    S.add("pe", [lambda e, k=k, t=t, sl=sl, q0=q0, dlt0=dlt0: e.matmul(
                        PS[:, 8 * k + t, :], lhsT=KT[sl][:, q0 + 64 * (dlt0 + 2 * t):q0 + 64 * (dlt0 + 2 * t) + 128],
                        rhs=QT[sl][:, q0:q0 + 128], start=True, stop=True) for t in range(ntl)],
                        reads=[("KT", sl), ("QT", sl)], writes=[("ps", k), ("po", k, 0)])
                    S.add("act", lambda e, k=k, ntl=ntl: e.activation(out=pT[k][:, 0:ntl, :], in_=PS[:, 8 * k:8 * k + ntl, :],
                                                                     func=AF.Exp, scale=0.125),
                          reads=[("ps", k)], writes=[("pT", k)])
                    S.add("dve", lambda e, k=k, ntl=ntl, jt=jt, wsl=wsl: e.tensor_tensor(
                        out=pT[k][:, 0:ntl, :], in0=pT[k][:, 0:ntl, :], in1=W[wsl][:, JT_OFF[jt]:JT_OFF[jt] + ntl, :],
                        op=ALU.mult), reads=[("pT", k), ("W", wsl)], writes=[("pT", k)])
                    def pv(k=k, sl=sl, q0=q0, dlt0=dlt0, ntl=ntl, poq=poq, gk=gk, gi=gi, rk=jn % 2, osl=osl, rp=rp):
                        S.add("pe", [lambda e, k=k, t=t, sl=sl, q0=q0, dlt0=dlt0, ntl=ntl, poq=poq: e.matmul(
                            PS[:, poq, :], lhsT=Vt[sl][:, (q0 + 64 * (dlt0 + 2 * t)) // 128, :], rhs=pT[k][:, t, :],
                            start=(t == 0), stop=(t == ntl - 1)) for t in range(ntl)],
                            reads=[("pT", k), ("Vt", sl)], writes=[("po", gk, gi)])
                        S.add("act", lambda e, rk=rk, poq=poq: e.activation(out=rz[rk][:, 0:128], in_=PS[64:128, poq, :], func=AF.Ln),
                              reads=[("po", gk, gi)], writes=[("rz", rk)])
                        S.add("act", lambda e, rk=rk: e.activation(out=rz[rk][:, 0:128], in_=rz[rk][:, 0:128], func=AF.Exp, scale=-1.0),
                              reads=[("rz", rk)], writes=[("rz", rk)])
                        S.add("dve", lambda e, rk=rk, osl=osl, rp=rp, poq=poq: e.tensor_tensor(
                            out=oo[osl][:, rp * 128:(rp + 1) * 128], in0=PS[0:64, poq, :], in1=rz[rk][:, 0:128], op=ALU.mult),
                            reads=[("po", gk, gi), ("rz", rk)], writes=[("oo", osl, rp)])

                    pend.append(pv)
                    if len(pend) > LAG:
                        pend.pop(0)()
                while pend:
                    pend.pop(0)()
                S.add("sp", lambda e, osl=osl, h=h, p=p, n=n: e.dma_start(
                    out=oT[256 + h * 64:256 + (h + 1) * 64, OOFF[p]:OOFF[p] + n], in_=oo[osl][:, 0:n]),
                    reads=[("oo", osl, i) for i in range(nrp)], writes=[("oT", 4 + h, p)], dma=True)
        S.flush()


def p5a_phase(nc, S, oT, gT, x1, w_a, w_b, w_o, post_g, x2):
    with ExitStack() as es:
        wa = es.enter_context(nc.sbuf_tensor("cwa", [128, 2, D], BF16))
        wb = es.enter_context(nc.sbuf_tensor("cwb", [128, 4, D], BF16))
        wo = es.enter_context(nc.sbuf_tensor("cwo", [128, 8, D], BF16))
        gpost = es.enter_context(nc.sbuf_tensor("cgpost", [128, D], F32))
        epsb = es.enter_context(nc.sbuf_tensor("cepsb", [128, 1], F32))
        with ExitStack() as es2:
            stg = [es2.enter_context(nc.sbuf_tensor("cstg%d" % i, [128, D], F32)) for i in range(3)]
            S.add("sp", lambda e: e.dma_start(out=gpost[:], in_=bcast_rows(post_g)), writes=["gpost"], dma=True)
            S.add("pool", lambda e: e.memset(epsb[:], EPS), writes=["epsb"])
            k = 0
            for (src, dst, nch) in ((w_a, wa, 2), (w_b, wb, 4), (w_o, wo, 8)):
                for c in range(nch):
                    sl = k % 3
                    S.add("sp", lambda e, src=src, c=c, sl=sl: e.dma_start(out=stg[sl][:], in_=src[c * 128:(c + 1) * 128, :]),
                          writes=[("stg", sl)], dma=True)
                    S.add("dve" if k % 2 else "act",
                          (lambda e, dst=dst, c=c, sl=sl: e.tensor_copy(out=dst[:, c, :], in_=stg[sl][:])) if k % 2 else
                          (lambda e, dst=dst, c=c, sl=sl: e.activation(out=dst[:, c, :], in_=stg[sl][:], func=AF.Copy)),
                          reads=[("stg", sl)], writes=[("w", id(dst), c)])
                    k += 1
            S.flush()
        os_ = [es.enter_context(nc.sbuf_tensor("cos%d" % i, [128, 6, ST], BF16)) for i in range(2)]
        gs = [es.enter_context(nc.sbuf_tensor("cgs%d" % i, [128, 16, ST], BF16)) for i in range(2)]
        uT = [es.enter_context(nc.sbuf_tensor("cuT%d" % i, [128, 8, ST], BF16)) for i in range(2)]
        t1 = [es.enter_context(nc.sbuf_tensor("ct1%d" % i, [128, ST], F32)) for i in range(2)]
        t2 = [es.enter_context(nc.sbuf_tensor("ct2%d" % i, [128, ST], F32)) for i in range(2)]
        xr = [es.enter_context(nc.sbuf_tensor("cxr%d" % i, [128, D], F32)) for i in range(3)]
        tts = [es.enter_context(nc.sbuf_tensor("ctt%d" % i, [128, D], F32)) for i in range(3)]
        junk = es.enter_context(nc.sbuf_tensor("cjunk", [128, D], BF16))
        sty = es.enter_context(nc.sbuf_tensor("csty", [128, 8], F32))
        pa = [es.enter_context(nc.psum_tensor("cpa%d" % i, [128, ST], F32)) for i in range(2)]
        pb = [es.enter_context(nc.psum_tensor("cpb%d" % i, [128, ST], F32)) for i in range(2)]
        py = [es.enter_context(nc.psum_tensor("cpy%d" % i, [128, D], F32)) for i in range(2)]
        S.add("dve", lambda e: e.tensor_scalar(out=gpost[:], in0=gpost[:], scalar1=1.0, scalar2=None, op0=ALU.mult),
              reads=["gpost"], writes=["gpost"])
        def stage1(s):
            sl = s % 2
            tb = s * ST
            S.add("sp", lambda e, sl=sl, tb=tb: e.dma_start(out=os_[sl][:], in_=oT[:, tb:tb + ST].rearrange("(c p) t -> p c t", p=128)),
                  reads=[("oT",)], writes=[("os", sl)], dma=True)
            S.add("sp", lambda e, sl=sl, tb=tb: e.dma_start(out=gs[sl][:], in_=gT[:, tb:tb + ST].rearrange("(c p) t -> p c t", p=128)),
                  reads=[("gT",)], writes=[("gs", sl)], dma=True)
            for dc in range(8):
                k = dc % 2
                S.add("pe", [lambda e, k=k, cc=cc, dc=dc, sl=sl: e.matmul(pa[k][:], lhsT=wa[:, cc, dc * 128:(dc + 1) * 128],
                                                                        rhs=os_[sl][:, cc, :], start=(cc == 0), stop=(cc == 1))
                             for cc in range(2)], reads=[("os", sl)], writes=[("pa", k)])
                S.add("pe", [lambda e, k=k, cc=cc, dc=dc, sl=sl: e.matmul(pb[k][:], lhsT=wb[:, cc, dc * 128:(dc + 1) * 128],
                                                                        rhs=os_[sl][:, 2 + cc, :], start=(cc == 0), stop=(cc == 3))
                             for cc in range(4)], reads=[("os", sl)], writes=[("pb", k)])
                S.add("dve", lambda e, k=k, dc=dc, sl=sl: e.tensor_tensor(out=t1[k][:], in0=pa[k][:], in1=gs[sl][:, dc, :], op=ALU.mult),
                      reads=[("pa", k), ("gs", sl)], writes=[("t1", k)])
                S.add("dve", lambda e, k=k, dc=dc, sl=sl: e.tensor_tensor(out=t2[k][:], in0=pb[k][:], in1=gs[sl][:, 8 + dc, :], op=ALU.mult),
                      reads=[("pb", k), ("gs", sl)], writes=[("t2", k)])
                S.add("pool", lambda e, dc=dc, sl=sl, k=k: e.tensor_tensor(out=uT[sl][:, dc, :], in0=t1[k][:], in1=t2[k][:], op=ALU.add),
                      reads=[("t1", k), ("t2", k)], writes=[("uT", sl)])

        def chain2(jj):
            xs = jj % 3
            ts = jj % 3
            c0 = (jj % 4) * 2
            S.add("act", lambda e, c0=c0, ts=ts: e.activation(out=junk[:], in_=tts[ts][:], func=AF.Square,
                                                             accum_out=sty[:, c0:c0 + 1]),
                  reads=[("tts", ts)], writes=["junk", ("sty", c0)])
            S.add("act", lambda e, c0=c0: e.activation(out=sty[:, c0 + 1:c0 + 2], in_=sty[:, c0:c0 + 1],
                                                     func=AF.Sqrt, scale=1.0 / D, bias=epsb[:, 0:1]),
                  reads=[("sty", c0), "epsb"], writes=[("sty", c0 + 1)])
            S.add("dve", lambda e, c0=c0: e.reciprocal(out=sty[:, c0 + 1:c0 + 2], in_=sty[:, c0 + 1:c0 + 2]),
                  reads=[("sty", c0 + 1)], writes=[("sty", c0 + 1)])
            S.add("dve", lambda e, c0=c0, ts=ts: e.scalar_tensor_tensor(out=tts[ts][:], in0=tts[ts][:], scalar=sty[:, c0 + 1:c0 + 2],
                                                                      in1=gpost[:], op0=ALU.mult, op1=ALU.mult),
                  reads=[("tts", ts), ("sty", c0 + 1), "gpost"], writes=[("tts", ts)])
            S.add("pool", lambda e, xs=xs, ts=ts: e.tensor_tensor(out=xr[xs][:], in0=tts[ts][:], in1=xr[xs][:], op=ALU.add),
                  reads=[("tts", ts), ("xr", xs)], writes=[("xr", xs)])
            S.add("sp", lambda e, xs=xs, jj=jj: e.dma_start(out=x2[jj * 128:(jj + 1) * 128, :], in_=xr[xs][:]),
                  reads=[("xr", xs)], writes=[("x2", jj)], dma=True)

        def stage2(s):
            sl = s % 2
            for j in range(4):
                jj = s * 4 + j
                xs = jj % 3
                ps_ = jj % 2
                S.add("sp", lambda e, jj=jj, xs=xs: e.dma_start(out=xr[xs][:], in_=x1[jj * 128:(jj + 1) * 128, :]),
                      reads=[("x1",)], writes=[("xr", xs)], dma=True)
                for half in range(2):
                    S.add("pe", [lambda e, c=c, j=j, half=half, ps_=ps_, sl=sl: e.matmul(
                        py[ps_][:, half * 512:(half + 1) * 512], lhsT=uT[sl][:, c, j * 128:(j + 1) * 128],
                        rhs=wo[:, c, half * 512:(half + 1) * 512], start=(c == 0), stop=(c == 7)) for c in range(8)],
                        reads=[("uT", sl)], writes=[("py", ps_, half)])
                S.add("dve", lambda e, ps_=ps_, xs=xs: e.tensor_copy(out=tts[xs][:], in_=py[ps_][:]),
                      reads=[("py", ps_, 0), ("py", ps_, 1)], writes=[("tts", xs)])
                if jj > 0:
                    chain2(jj - 1)

        nblk = NO // ST
        stage1(0)
        for s in range(nblk):
            if s + 1 < nblk:
                stage1(s + 1)
            stage2(s)
        chain2(NO // 128 - 1)
        S.flush()


def build_program(debug=None, force_internal=False):
    nc = bass.Bass("TRN2", target_bir_lowering=False)

    def din(name, shape, dt=F32):
        return nc.dram_tensor(name, list(shape), dt, kind="ExternalInput").ap()

    def dscr(name, shape, dt, out=False):
        return nc.dram_tensor(name, list(shape), dt, kind=("ExternalOutput" if out else "Internal")).ap()

    dbg = (debug is not None) and not force_internal
    xe = din("xe", [TE, D])
    f1g = din("ffn1_w_gate", [D, DFF]); f1u = din("ffn1_w_up", [D, DFF]); f1d = din("ffn1_w_down", [DFF, D])
    f1pre = din("ffn1_pre_g", [D]); f1post = din("ffn1_post_g", [D])
    f2g = din("ffn2_w_gate", [D, DFF]); f2u = din("ffn2_w_up", [D, DFF]); f2d = din("ffn2_w_down", [DFF, D])
    f2pre = din("ffn2_pre_g", [D]); f2post = din("ffn2_post_g", [D])
    mixpre = din("mix_pre_g", [D]); mixpost = din("mix_post_g", [D])
    w_in = din("w_in", [D, INW]); b_gate = din("b_gate", [2048])
    w_a = din("w_branch_a", [256, D]); w_b = din("w_branch_b", [512, D]); w_o = din("w_out", [D, D])
    rope = din("rope", [2, 128, TE])
    vexp = din("vexp", [128, 3, TE // 128, 64], BF16)
    bmA = din("bmA", [128, 256], BF16)
    bias4 = din("bias4", [8, 8, 128, 128])
    mB = din("mB", [128, 27, 128], BF16)
    x1 = dscr("x1", [NO, D], F32, out=dbg)
    h2 = dscr("h2", [TE, D], BF16, out=dbg)
    qkT = dscr("qkT", [40 * 64, TE], BF16, out=dbg)
    vtA = dscr("vtA", [3, TE // 128, 128, 256], BF16, out=dbg)
    vtB = dscr("vtB", [TE // 128, 128, 512], BF16, out=dbg)
    gT = dscr("gT", [2048, NO], BF16, out=dbg)
    oT = dscr("oT", [768, NO], BF16, out=dbg)
    x2 = dscr("x2", [NO, D], F32, out=dbg)
    yout = dscr("yout", [NO, D], F32, out=True)

    with ExitStack() as es:
        S = Sched(nc, es)

        def out_p1(S, jj, xt, key, bufs):
            t0 = jj * 128
            own = own_block(t0 - t0 % ST)
            if own is not None:
                own += t0 % ST
                S.add("sp", lambda e, own=own: e.dma_start(out=x1[own:own + 128, :], in_=xt[:]),
                      reads=[key], writes=[("x1", own)], dma=True)
            ob, st_o, hbb, epsb = bufs["ob"], bufs["st_o"], bufs["hb"], bufs["epsb"]
            junk = hbb[jj % 4]
            sl = jj % 2
            c0 = (jj % 4) * 2
            S.add("act", lambda e, c0=c0: e.activation(out=junk[:], in_=xt[:], func=AF.Square,
                                                     accum_out=st_o[:, c0:c0 + 1]),
                  reads=[key], writes=[("hb", jj % 4), ("sto", c0)])
            S.add("act", lambda e, c0=c0: e.activation(out=st_o[:, c0 + 1:c0 + 2], in_=st_o[:, c0:c0 + 1],
                                                     func=AF.Sqrt, scale=1.0 / D, bias=epsb[:, 0:1]),
                  reads=[("sto", c0)], writes=[("sto", c0 + 1)])
            S.add("dve", lambda e, c0=c0: e.reciprocal(out=st_o[:, c0 + 1:c0 + 2], in_=st_o[:, c0 + 1:c0 + 2]),
                  reads=[("sto", c0 + 1)], writes=[("sto", c0 + 1)])
            S.add("act", lambda e, c0=c0, sl=sl: e.activation(out=ob[sl][:], in_=xt[:], func=AF.Copy,
                                                             scale=st_o[:, c0 + 1:c0 + 2]),
                  reads=[key, ("sto", c0 + 1)], writes=[("ob", sl)])
            S.add("sp", lambda e, sl=sl, t0=t0: e.dma_start(out=h2[t0:t0 + 128, :], in_=ob[sl][:]),
                  reads=[("ob", sl)], writes=[("h2", t0)], dma=True)

        def out_p5(S, jj, xt, key, bufs):
            S.add("sp", lambda e, jj=jj: e.dma_start(out=yout[jj * 128:(jj + 1) * 128, :], in_=xt[:]),
                  reads=[key], writes=[("yout", jj)], dma=True)

        ph = debug or "all"
        if ph in ("all", "p1", "p1s"):
            ffn_phase(nc, S, "f1", xe, TE if ph != "p1s" else 1024, f1g, f1u, f1d, f1pre, f1post, out_p1)
        if ph in ("all", "p2"):
            p2_phase(nc, S, h2, w_in, mixpre, b_gate, rope, qkT, vtA, vtB, gT, TE // MT)
        if ph in ("all", "p3"):
            p3_phase(nc, S, qkT, vtA, vexp, bmA, oT)
        if ph in ("all", "p4"):
            p4_phase(nc, S, qkT, vtB, bias4, mB, oT)
        if ph in ("all", "p5"):
            p5a_phase(nc, S, oT, gT, x1, w_a, w_b, w_o, mixpost, x2)
            ffn_phase(nc, S, "f2", x2, NO, f2g, f2u, f2d, f2pre, f2post, out_p5)
        sched_finish(S)
    return nc


ROPE_THETA = 500000.0


def host_tables(q):
    f32 = np.float32
    inv_freq = (f32(ROPE_THETA) ** (-np.arange(0, 16, 2, dtype=f32) / f32(16))).astype(f32)
    rope = np.zeros((2, 128, TE), f32)
    kbA = np.zeros((128, 3, TE // 128), f32)
    for p, n in enumerate(PIECES):
        ext = n + 2 * HALO
        pos = (q * n - HALO + np.arange(ext)).astype(f32)
        ang = pos[:, None] * inv_freq[None, :]
        cos = np.cos(ang).astype(f32).T
        sin = np.sin(ang).astype(f32).T
        c16 = np.concatenate([cos, cos], 0)
        s16 = np.concatenate([-sin, sin], 0)
        rope[0, :, EOFF[p]:EOFF[p] + ext] = np.tile(c16, (8, 1))
        rope[1, :, EOFF[p]:EOFF[p] + ext] = np.tile(s16, (8, 1))
        seqlen = 4 * n
        for g, d in enumerate((1, 4, 16)):
            for idx in range(ext // 128):
                j, r = idx // d, idx % d
                te = d * (128 * j + np.arange(128)) + r
                ps = q * n - HALO + te
                kbA[:, g, EOFF[p] // 128 + idx] = np.where((ps >= 0) & (ps < seqlen), 1.0, 0.0)
    kl = np.arange(128)[:, None]
    ql = np.arange(128)[None, :]
    bmA = np.concatenate([(kl >= ql), (kl <= ql)], 1).astype(f32).astype(ml_dtypes.bfloat16)
    nrows = 32
    rows_seq = 4 * nrows
    mB = np.zeros((128, 27, 128), f32)
    kr_l, kc = np.arange(128)[:, None] // 64, np.arange(128)[:, None] % 64
    qr_l, qc = np.arange(128)[None, :] // 64, np.arange(128)[None, :] % 64
    cs = np.clip(qc - 8, 0, 48)
    colv = (kc >= cs) & (kc < cs + 16)
    for jt in range(5):
        b0, ntl = JT_TILES[jt]
        r = {0: 8, 1: 0, 2: 2, 3: nrows - 4, 4: nrows - 2}[jt]
        for t in range(ntl):
            dlt = -8 + 2 * (b0 + t)
            qrow = q * nrows + r + qr_l
            krow = q * nrows + r + dlt + kr_l
            rs = np.clip(qrow - 4, 0, rows_seq - 8)
            rowv = (krow >= rs) & (krow < rs + 8)
            mB[:, JT_OFF[jt] + t, :] = (rowv & colv)
    vexp = np.ascontiguousarray(np.broadcast_to(kbA[:, :, :, None], (128, 3, TE // 128, 64))).astype(ml_dtypes.bfloat16)
    return dict(rope=rope, vexp=vexp, bmA=bmA, mB=mB.astype(ml_dtypes.bfloat16))


def gather_bias4(rpb):
    kr_l, kc = np.arange(128)[:, None] // 64, np.arange(128)[:, None] % 64
    qr_l, qc = np.arange(128)[None, :] // 64, np.arange(128)[None, :] % 64
    dc = np.clip(kc - qc + 15, 0, 30)
    out = np.zeros((8, 8, 128, 128), np.float32)
    for bi in range(8):
        dr = -8 + 2 * bi + kr_l - qr_l
        dri = np.clip(dr + 7, 0, 14)
        out[:, bi] = rpb[:, dri, dc]
    return out


def shard_inputs(inputs):
    xp = np.asarray(inputs["x_prompt"], np.float32)
    xs = np.asarray(inputs["x_sample"], np.float32)
    wnames = ("ffn1_w_gate", "ffn1_w_up", "ffn1_w_down", "ffn1_pre_g", "ffn1_post_g", "mix_pre_g", "mix_post_g",
              "w_in", "b_gate", "w_branch_a", "w_branch_b", "w_out", "ffn2_pre_g", "ffn2_post_g",
              "ffn2_w_gate", "ffn2_w_up", "ffn2_w_down")
    shared = {k: np.ascontiguousarray(np.asarray(inputs[k], np.float32)[0]) for k in wnames}
    shared["bias4"] = gather_bias4(np.asarray(inputs["rpb"], np.float32)[0])
    tabs = [host_tables(q) for q in range(4)]
    maps = []
    for c in range(NCORES):
        b, q = c // 4, c % 4
        xe = np.zeros((TE, D), np.float32)
        for p, (src, n) in enumerate(((xp[b], PIECES[0]), (xs[b], PIECES[1]))):
            Sq = src.shape[0]
            lo = q * n - HALO
            hi = q * n + n + HALO
            a, bnd = max(lo, 0), min(hi, Sq)
            xe[EOFF[p] + (a - lo):EOFF[p] + (bnd - lo)] = src[a:bnd]
        m = {"xe": xe}
        m.update(shared)
        m.update(tabs[q])
        maps.append(m)
    return maps


_NC_CACHE = {}


def kernel(**inputs):
    maps = shard_inputs(inputs)
    if "nc" not in _NC_CACHE:
        _NC_CACHE["nc"] = build_program()
    res = run_bass_kernel_spmd(_NC_CACHE["nc"], maps, core_ids=list(range(NCORES)))
    B, SQ = np.asarray(inputs["x_prompt"]).shape[:2]
    DB, DS = np.asarray(inputs["x_sample"]).shape[:2]
    yp = np.zeros((B, SQ, D), np.float32)
    ys = np.zeros((DB, DS, D), np.float32)
    for c in range(NCORES):
        b, q = c // 4, c % 4
        y = np.asarray(res.results[c]["yout"], np.float32)
        yp[b, q * PIECES[0]:(q + 1) * PIECES[0]] = y[0:PIECES[0]]
        ys[b, q * PIECES[1]:(q + 1) * PIECES[1]] = y[PIECES[0]:NO]
    return (yp, ys)
```

```python
from contextlib import ExitStack
import numpy as np
import ml_dtypes
import concourse.bass as bass
import concourse.mybir as mybir
from concourse.bass_utils import run_bass_kernel_spmd

F32 = mybir.dt.float32
BF16 = mybir.dt.bfloat16
ALU = mybir.AluOpType
AF = mybir.ActivationFunctionType

D = 1024
DFF = 2816
NF = DFF // 128
HALO = 1024
PIECES = (2048, 4096)
EOFF = (0, 2048 + 2 * HALO)
OOFF = (0, 2048)
TE = sum(n + 2 * HALO for n in PIECES)
NO = sum(PIECES)
ST = 512
EPS = 1e-6
NCORES = 8
INW = 5888

ENGS = ("pe", "act", "dve", "pool", "sp")
STRICT = True
NDS = 24


class Sched:
    def __init__(self, nc, es):
        self.nc = nc
        self.sem = {e: es.enter_context(nc.semaphore("s_" + e)) for e in ENGS}
        self.dsem = [es.enter_context(nc.semaphore("d%d" % i)) for i in range(NDS)]
        self.bar = es.enter_context(nc.semaphore("bar"))
        self.fin = es.enter_context(nc.semaphore("fin"))
        self.nbar = 0
        self.dcum = [0] * NDS
        self.dnext = 0
        self.cnt = {e: 0 for e in ENGS}
        self.waited = {e: {} for e in ENGS}
        self.stream = {e: [] for e in ENGS}
        self.lastw = {}
        self.readers = {}

    def _semh(self, key):
        return self.sem[key[1]] if key[0] == "e" else self.dsem[key[1]]

    def add(self, eng, fns, reads=(), writes=(), dma=False):
        if not isinstance(fns, (list, tuple)):
            fns = [fns]
        deps = set()
        for r in reads:
            if r in self.lastw:
                deps.add(self.lastw[r])
        for w in writes:
            if w in self.lastw:
                deps.add(self.lastw[w])
            for t in self.readers.get(w, ()):
                deps.add(t)
        if dma:
            k = self.dnext
            self.dnext = (k + 1) % NDS
            if self.dcum[k] > 0:
                deps.add((("d", k), self.dcum[k]))
            self.dcum[k] += 16
            token = (("d", k), self.dcum[k])
            inc = (self.dsem[k], 16)
        else:
            self.cnt[eng] += 1
            token = (("e", eng), self.cnt[eng])
            inc = (self.sem[eng], 1)
        need = {}
        for key, val in deps:
            if key == ("e", eng) and not dma and (eng == "pe" or not STRICT):
                continue
            if val > need.get(key, 0):
                need[key] = val
        for key, val in need.items():
            if self.waited[eng].get(key, 0) >= val:
                continue
            self.waited[eng][key] = val
            self.stream[eng].append(("w", self._semh(key), val))
        self.stream[eng].append(("o", list(fns), inc))
        for r in reads:
            self.readers.setdefault(r, []).append(token)
        for w in writes:
            self.lastw[w] = token
            self.readers[w] = []
        return token

    def flush(self):
        nc = self.nc
        self.nbar += 1
        target = self.nbar * len(ENGS)
        for e in ("sp", "pool", "act"):
            for k in range(NDS):
                if self.dcum[k] > self.waited[e].get(("d", k), 0):
                    self.waited[e][("d", k)] = self.dcum[k]
                    self.stream[e].append(("w", self.dsem[k], self.dcum[k]))
        streams = self.stream
        bar = self.bar

        def run(eng_obj, ops):
            for op in ops:
                if op[0] == "w":
                    eng_obj.wait_ge(op[1], op[2])
                else:
                    last = None
                    for fn in op[1]:
                        last = fn(eng_obj)
                    last.then_inc(op[2][0], op[2][1])
            eng_obj.sem_inc(bar, 1)
            eng_obj.wait_ge(bar, target)

        with nc.Block() as block:
            @block.tensor
            def _(e):
                run(e, streams["pe"])

            @block.scalar
            def _(e):
                run(e, streams["act"])

            @block.vector
            def _(e):
                run(e, streams["dve"])

            @block.gpsimd
            def _(e):
                run(e, streams["pool"])

            @block.sync
            def _(e):
                run(e, streams["sp"])
        self.stream = {e: [] for e in ENGS}
        self.lastw = {}
        self.readers = {}


def sched_finish(S, es_unused=None):
    nc = S.nc
    fin = S.fin
    allsems = list(S.sem.values()) + list(S.dsem) + [S.bar]
    with nc.Block() as block:
        @block.tensor
        def _(e):
            e.sem_inc(fin, 1)

        @block.scalar
        def _(e):
            e.sem_inc(fin, 1)

        @block.vector
        def _(e):
            e.sem_inc(fin, 1)

        @block.gpsimd
        def _(e):
            e.sem_inc(fin, 1)

        @block.sync
        def _(e):
            e.wait_ge(fin, 4)
            for sm in allsems:
                e.sem_clear(sm)
            e.sem_clear(fin)


def make_identity(S, idn, idf):
    S.add("pool", lambda e: e.iota(idf[:], [[1, 128]], channel_multiplier=-1, allow_small_or_imprecise_dtypes=True),
          writes=["idf"])
    S.add("pool", lambda e: e.tensor_single_scalar(out=idf[:], in_=idf[:], scalar=0.0, op=ALU.is_equal),
          reads=["idf"], writes=["idf"])
    S.add("pool", lambda e: e.tensor_copy(out=idn[:], in_=idf[:]), reads=["idf"], writes=["idn"])


def bcast_rows(ap_1d, nparts=128):
    return bass.AP(ap_1d.tensor, ap_1d.offset, [[0, nparts]] + [list(x) for x in ap_1d.ap])


def ffn_phase(nc, S, name, x_src, ntok, w_gate, w_up, w_down, pre_g, post_g, emit_out):
    nst = ntok // ST
    with ExitStack() as es:
        wg = es.enter_context(nc.sbuf_tensor(name + "wg", [128, 8, DFF], BF16))
        wu = es.enter_context(nc.sbuf_tensor(name + "wu", [128, 8, DFF], BF16))
        wd = es.enter_context(nc.sbuf_tensor(name + "wd", [128, NF, D], BF16))
        gpost = es.enter_context(nc.sbuf_tensor(name + "gpost", [128, D], F32))
        gpre = es.enter_context(nc.sbuf_tensor(name + "gpre", [128, 8], F32))
        epsb = es.enter_context(nc.sbuf_tensor(name + "epsb", [128, 1], F32))
        idn = es.enter_context(nc.sbuf_tensor(name + "idn", [128, 128], BF16))
        with ExitStack() as es2:
            stg = [es2.enter_context(nc.sbuf_tensor(name + "stg%d" % i, [128, DFF], F32)) for i in range(3)]
            idf = es2.enter_context(nc.sbuf_tensor(name + "idf", [128, 128], F32))
            S.add("sp", lambda e: e.dma_start(out=gpost[:], in_=bcast_rows(post_g)), writes=["gpost"], dma=True)
            S.add("sp", lambda e: e.dma_start(out=gpre[:], in_=pre_g.rearrange("(c p) -> p c", p=128),
                                              allow_slow_non_contiguous=True), writes=["gpre"], dma=True)
            S.add("pool", lambda e: e.memset(epsb[:], EPS), writes=["epsb"])
            make_identity(S, idn, idf)
            S.add("dve", lambda e: e.tensor_scalar(out=gpost[:], in0=gpost[:], scalar1=0.5, scalar2=None,
                                                   op0=ALU.mult), reads=["gpost"], writes=["gpost"])
            k = 0
            cv = ("act", "dve", "pool")
            for (wsrc, wdst, nch, width, scaled) in ((w_gate, wg, 8, DFF, True), (w_up, wu, 8, DFF, True),
                                                     (w_down, wd, NF, D, False)):
                for c in range(nch):
                    sl = k % 3
                    S.add("sp", lambda e, wsrc=wsrc, c=c, sl=sl, width=width: e.dma_start(
                        out=stg[sl][:, 0:width], in_=wsrc[c * 128:(c + 1) * 128, :]),
                        writes=[("stg", sl)], dma=True)
                    eng = "act" if scaled else "dve"
                    rd = [("stg", sl)] + (["gpre"] if scaled else [])
                    if scaled:
                        if eng == "act":
                            fn = lambda e, wdst=wdst, c=c, sl=sl, width=width: e.activation(
                                out=wdst[:, c, :], in_=stg[sl][:, 0:width], func=AF.Copy, scale=gpre[:, c:c + 1])
                        else:
                            fn = lambda e, wdst=wdst, c=c, sl=sl, width=width: e.tensor_scalar(
                                out=wdst[:, c, :], in0=stg[sl][:, 0:width], scalar1=gpre[:, c:c + 1], scalar2=None,
                                op0=ALU.mult)
                    else:
                        if eng == "act":
                            fn = lambda e, wdst=wdst, c=c, sl=sl, width=width: e.activation(
                                out=wdst[:, c, :], in_=stg[sl][:, 0:width], func=AF.Copy)
                        else:
                            fn = lambda e, wdst=wdst, c=c, sl=sl, width=width: e.tensor_copy(
                                out=wdst[:, c, :], in_=stg[sl][:, 0:width])
                    S.add(eng, fn, reads=rd, writes=[("w", id(wdst), c)])
                    k += 1
            S.flush()
        xa = [es.enter_context(nc.sbuf_tensor(name + "xa%d" % i, [128, D], F32)) for i in range(2)]
        xr = [es.enter_context(nc.sbuf_tensor(name + "xr%d" % i, [128, D], F32)) for i in range(2)]
        tt = [es.enter_context(nc.sbuf_tensor(name + "tt%d" % i, [128, D], F32)) for i in range(2)]
        hb = [es.enter_context(nc.sbuf_tensor(name + "hb%d" % i, [128, D], BF16)) for i in range(4)]
        hT = es.enter_context(nc.sbuf_tensor(name + "hT", [128, 8, ST], BF16))
        aT = es.enter_context(nc.sbuf_tensor(name + "aT", [128, NF, ST], BF16))
        sg = [es.enter_context(nc.sbuf_tensor(name + "sg%d" % i, [128, ST], F32)) for i in range(2)]
        ob = [es.enter_context(nc.sbuf_tensor(name + "ob%d" % i, [128, D], BF16)) for i in range(2)]
        st_a = es.enter_context(nc.sbuf_tensor(name + "sta", [128, 8], F32))
        st_y = es.enter_context(nc.sbuf_tensor(name + "sty", [128, 32], F32))
        st_o = es.enter_context(nc.sbuf_tensor(name + "sto", [128, 8], F32))
        pt = [es.enter_context(nc.psum_tensor(name + "pt%d" % i, [128, 8, 128], BF16)) for i in range(2)]
        pg = [es.enter_context(nc.psum_tensor(name + "pg%d" % i, [128, ST], F32)) for i in range(2)]
        pu = [es.enter_context(nc.psum_tensor(name + "pu%d" % i, [128, ST], F32)) for i in range(2)]
        py = es.enter_context(nc.psum_tensor(name + "py", [128, D], F32))
        bufs = dict(ob=ob, st_o=st_o, hb=hb, epsb=epsb)

        def stage_a_pre(s):
            for j in range(4):
                jj = s * 4 + j
                sl = jj % 2
                S.add("sp", lambda e, jj=jj, sl=sl: e.dma_start(out=xa[sl][:], in_=x_src[jj * 128:(jj + 1) * 128, :]),
                      writes=[("xa", sl)], dma=True)
                c0 = (jj % 4) * 2
                S.add("act", lambda e, sl=sl, c0=c0, j=j: e.activation(out=hb[j][:], in_=xa[sl][:], func=AF.Square,
                                                                    accum_out=st_a[:, c0:c0 + 1]),
                      reads=[("xa", sl)], writes=[("hb", j), ("sta", c0)])
                S.add("act", lambda e, c0=c0: e.activation(out=st_a[:, c0 + 1:c0 + 2], in_=st_a[:, c0:c0 + 1],
                                                         func=AF.Sqrt, scale=1.0 / D, bias=epsb[:, 0:1]),
                      reads=[("sta", c0)], writes=[("sta", c0 + 1)])
                S.add("dve", lambda e, c0=c0: e.reciprocal(out=st_a[:, c0 + 1:c0 + 2], in_=st_a[:, c0 + 1:c0 + 2]),
                      reads=[("sta", c0 + 1)], writes=[("sta", c0 + 1)])
                S.add("dve", lambda e, sl=sl, c0=c0, j=j: e.tensor_scalar(out=hb[j][:], in0=xa[sl][:],
                                                                         scalar1=st_a[:, c0 + 1:c0 + 2], scalar2=None,
                                                                         op0=ALU.mult),
                      reads=[("xa", sl), ("sta", c0 + 1)], writes=[("hb", j)])

        def stage_a_T(s):
            for j in range(4):
                sl = j % 2
                S.add("pe", [lambda e, sl=sl, c=c, j=j: e.transpose(pt[sl][:, c, :], hb[j][:, c * 128:(c + 1) * 128], idn[:])
                             for c in range(8)], reads=[("hb", j)], writes=[("pt", sl)])
                S.add("act", lambda e, sl=sl, j=j: e.activation(out=hT[:, :, j * 128:(j + 1) * 128], in_=pt[sl][:],
                                                               func=AF.Copy),
                      reads=[("pt", sl)], writes=[("hT", j)])

        def stage_gu(s, f0, f1):
            for f in range(f0, f1):
                sl = f % 2
                S.add("pe", [lambda e, c=c, f=f, sl=sl: e.matmul(pg[sl][:], lhsT=wg[:, c, f * 128:(f + 1) * 128],
                                                                rhs=hT[:, c, :], start=(c == 0), stop=(c == 7))
                             for c in range(8)], reads=[("hT", j) for j in range(4)], writes=[("pg", sl)])
                S.add("pe", [lambda e, c=c, f=f, sl=sl: e.matmul(pu[sl][:], lhsT=wu[:, c, f * 128:(f + 1) * 128],
                                                                rhs=hT[:, c, :], start=(c == 0), stop=(c == 7))
                             for c in range(8)], reads=[("hT", j) for j in range(4)], writes=[("pu", sl)])
                S.add("act", lambda e, sl=sl: e.activation(out=sg[sl][:], in_=pg[sl][:], func=AF.Silu),
                      reads=[("pg", sl)], writes=[("sg", sl)])
                S.add("dve", lambda e, sl=sl, f=f: e.tensor_tensor(out=aT[:, f, :], in0=sg[sl][:], in1=pu[sl][:],
                                                                  op=ALU.mult),
                      reads=[("sg", sl), ("pu", sl)], writes=[("aT", f)])

        def down_chain(jj, j):
            sl = jj % 2
            ts = jj % 2
            c0 = (jj % 4) * 8
            S.add("act", lambda e, c0=c0, ts=ts, j=j: e.activation(out=hb[j][:], in_=tt[ts][:], func=AF.Square,
                                                               accum_out=st_y[:, c0 + 4:c0 + 5]),
                  reads=[("tt", ts, 0), ("tt", ts, 1)], writes=[("hb", j), ("sty", c0 + 4)])
            S.add("act", lambda e, c0=c0: e.activation(out=st_y[:, c0 + 6:c0 + 7], in_=st_y[:, c0 + 4:c0 + 5],
                                                     func=AF.Sqrt, scale=1.0 / D, bias=epsb[:, 0:1]),
                  reads=[("sty", c0 + 4)], writes=[("sty", c0 + 6)])
            S.add("dve", lambda e, c0=c0: e.reciprocal(out=st_y[:, c0 + 6:c0 + 7], in_=st_y[:, c0 + 6:c0 + 7]),
                  reads=[("sty", c0 + 6)], writes=[("sty", c0 + 6)])
            S.add("dve", lambda e, c0=c0, ts=ts: e.scalar_tensor_tensor(out=tt[ts][:], in0=tt[ts][:], scalar=st_y[:, c0 + 6:c0 + 7],
                                                                      in1=gpost[:], op0=ALU.mult, op1=ALU.mult),
                  reads=[("tt", ts, 0), ("tt", ts, 1), ("sty", c0 + 6)], writes=[("tt", ts, 0), ("tt", ts, 1)])
            S.add("dve", lambda e, sl=sl, ts=ts: e.tensor_tensor(out=xr[sl][:], in0=tt[ts][:], in1=xr[sl][:], op=ALU.add),
                  reads=[("tt", ts, 0), ("tt", ts, 1), ("xr", sl)], writes=[("xr", sl)])
            emit_out(S, jj, xr[sl], ("xr", sl), bufs)

        def stage_down(s):
            for j in range(4):
                jj = s * 4 + j
                sl = jj % 2
                ts = jj % 2
                S.add("sp", lambda e, jj=jj, sl=sl: e.dma_start(out=xr[sl][:], in_=x_src[jj * 128:(jj + 1) * 128, :]),
                      writes=[("xr", sl)], dma=True)
                for half in range(2):
                    S.add("pe", [lambda e, f=f, j=j, half=half: e.matmul(
                        py[:, half * 512:(half + 1) * 512], lhsT=aT[:, f, j * 128:(j + 1) * 128],
                        rhs=wd[:, f, half * 512:(half + 1) * 512], start=(f == 0), stop=(f == NF - 1))
                        for f in range(NF)], reads=[("aT", f) for f in range(NF)], writes=[("py", half)])
                    S.add("dve", lambda e, half=half, ts=ts: e.tensor_copy(out=tt[ts][:, half * 512:(half + 1) * 512],
                                                                           in_=py[:, half * 512:(half + 1) * 512]),
                          reads=[("py", half)], writes=[("tt", ts, half)])
                if j > 0:
                    down_chain(jj - 1, j - 1)
            down_chain(s * 4 + 3, 3)

        stage_a_pre(0)
        stage_a_T(0)
        for s in range(nst):
            stage_gu(s, 0, 10)
            if s + 1 < nst:
                stage_a_pre(s + 1)
            stage_gu(s, 10, NF)
            if s + 1 < nst:
                stage_a_T(s + 1)
            stage_down(s)
        S.flush()


MT = 2048
HK_A, HQ_A, HK_B, HQ_B = 0, 12, 24, 32
COL = dict(QA=0, KA=768, VA=1536, QB=2304, KB=2816, VB=3328, G=3840)


def own_block(t0):
    for p, n in enumerate(PIECES):
        lo = EOFF[p] + HALO
        if lo <= t0 < lo + n:
            return OOFF[p] + (t0 - lo)
    return None


def p2_phase(nc, S, h2, w_in, mixpre, b_gate, rope, qkT, vtA, vtB, gT, nmt):
    with ExitStack() as es:
        win = es.enter_context(nc.sbuf_tensor("win", [128, 8, INW], BF16))
        wr = es.enter_context(nc.sbuf_tensor("wr", [128, 8, 384], BF16))
        wrp = es.enter_context(nc.sbuf_tensor("wrp", [128, 8, 384], BF16))
        gmix = es.enter_context(nc.sbuf_tensor("gmix", [128, 8], F32))
        bg = es.enter_context(nc.sbuf_tensor("bg", [128, 16], F32))
        idn = es.enter_context(nc.sbuf_tensor("p2idn", [128, 128], BF16))
        with ExitStack() as es2:
            stg = [es2.enter_context(nc.sbuf_tensor("p2stg%d" % i, [128, INW], F32)) for i in range(2)]
            idf = es2.enter_context(nc.sbuf_tensor("p2idf", [128, 128], F32))
            S.add("sp", lambda e: e.dma_start(out=gmix[:], in_=mixpre.rearrange("(c p) -> p c", p=128),
                                              allow_slow_non_contiguous=True), writes=["gmix"], dma=True)
            S.add("sp", lambda e: e.dma_start(out=bg[:], in_=b_gate.rearrange("(c p) -> p c", p=128),
                                              allow_slow_non_contiguous=True), writes=["bg"], dma=True)
            make_identity(S, idn, idf)
            for c in range(8):
                sl = c % 2
                S.add("sp", lambda e, c=c, sl=sl: e.dma_start(out=stg[sl][:], in_=w_in[c * 128:(c + 1) * 128, :]),
                      writes=[("stg", sl)], dma=True)
                sc = gmix[:, c:c + 1]
                S.add("act", lambda e, c=c, sl=sl, sc=sc: e.activation(out=win[:, c, 0:3072], in_=stg[sl][:, 0:3072],
                                                                      func=AF.Copy, scale=sc),
                      reads=[("stg", sl), "gmix"], writes=[("win", c, 0)])
                S.add("act", lambda e, c=c, sl=sl, sc=sc: e.activation(out=win[:, c, 3072:INW], in_=stg[sl][:, 3072:INW],
                                                                      func=AF.Copy, scale=sc),
                      reads=[("stg", sl), "gmix"], writes=[("win", c, 1)])
                fns = []
                for (dst0, src0) in ((0, COL["KA"]), (192, COL["QA"])):
                    def mk(dst_t, dlo, slo, dst0=dst0, src0=src0, c=c, sl=sl, sc=sc):
                        def fn(e):
                            src = stg[sl][:, src0:src0 + 768].rearrange("p (h e) -> p h e", e=64)[:, :, slo:slo + (16 if dst_t is wr else 8)]
                            n = 16 if dst_t is wr else 8
                            dst = dst_t[:, c, dst0:dst0 + 192].rearrange("p (h e) -> p h e", e=16)[:, :, dlo:dlo + n]
                            return e.activation(out=dst, in_=src, func=AF.Copy, scale=sc)
                        return fn
                    fns.append(mk(wr, 0, 0))
                    fns.append(mk(wrp, 0, 8))
                    fns.append(mk(wrp, 8, 0))
                S.add("act", fns, reads=[("stg", sl), "gmix"], writes=[("wr", c)])
            S.flush()
        hb = [es.enter_context(nc.sbuf_tensor("p2hb%d" % i, [128, D], BF16)) for i in range(2)]
        hT = es.enter_context(nc.sbuf_tensor("p2hT", [128, 8, MT], BF16))
        fo = [es.enter_context(nc.sbuf_tensor("p2fo%d" % i, [128, 512], BF16)) for i in range(4)]
        cs = [es.enter_context(nc.sbuf_tensor("p2cs%d" % i, [128, 2, 512], F32)) for i in range(2)]
        r1 = es.enter_context(nc.sbuf_tensor("p2r1", [128, 512], F32))
        r2 = es.enter_context(nc.sbuf_tensor("p2r2", [128, 512], F32))
        ro = [es.enter_context(nc.sbuf_tensor("p2ro%d" % i, [128, 512], BF16)) for i in range(2)]
        vo = [es.enter_context(nc.sbuf_tensor("p2vo%d" % i, [128, 512], BF16)) for i in range(2)]
        pt = [es.enter_context(nc.psum_tensor("p2pt%d" % i, [128, 8, 128], BF16)) for i in range(2)]
        pf = [es.enter_context(nc.psum_tensor("p2pf%d" % i, [128, 512], F32)) for i in range(4)]
        pv = [es.enter_context(nc.psum_tensor("p2pv%d" % i, [128, 512], F32)) for i in range(2)]
        cnt = dict(f=0, v=0, fo=0, ro=0, cs=0)
        qk2 = qkT

        def fm_tile(cols0, hTsl):
            k = cnt["f"] % 4
            cnt["f"] += 1
            S.add("pe", [lambda e, c=c, k=k: e.matmul(pf[k][:], lhsT=cols0(c), rhs=hTsl(c), start=(c == 0), stop=(c == 7))
                         for c in range(8)], reads=["hT"], writes=[("pf", k)])
            return k

        for mt in range(nmt):
            T0 = mt * MT
            for j in range(MT // 128):
                sl = j % 2
                t0 = T0 + j * 128
                S.add("sp", lambda e, t0=t0, sl=sl: e.dma_start(out=hb[sl][:], in_=h2[t0:t0 + 128, :]),
                      writes=[("hb", sl)], dma=True)
                S.add("pe", [lambda e, sl=sl, c=c: e.transpose(pt[sl][:, c, :], hb[sl][:, c * 128:(c + 1) * 128], idn[:])
                             for c in range(8)], reads=[("hb", sl)], writes=[("pt", sl)])
                eng = "act" if j % 2 == 0 else "dve"
                if eng == "act":
                    fn = lambda e, sl=sl, j=j: e.activation(out=hT[:, :, j * 128:(j + 1) * 128], in_=pt[sl][:], func=AF.Copy)
                else:
                    fn = lambda e, sl=sl, j=j: e.tensor_copy(out=hT[:, :, j * 128:(j + 1) * 128], in_=pt[sl][:])
                S.add(eng, fn, reads=[("pt", sl)], writes=["hT"])
            for b in range(MT // ST):
                tb = T0 + b * ST
                own = own_block(tb)
                hsl = lambda c, b=b: hT[:, c, b * ST:(b + 1) * ST]
                jobs = [(COL["KA"] + 128 * i, (HK_A * 64) + 128 * i) for i in range(6)]
                jobs += [(COL["KB"] + 128 * i, (HK_B * 64) + 128 * i) for i in range(4)]
                if own is not None:
                    jobs += [(COL["QA"] + 128 * i, (HQ_A * 64) + 128 * i) for i in range(6)]
                    jobs += [(COL["QB"] + 128 * i, (HQ_B * 64) + 128 * i) for i in range(4)]
                for (wc, row0) in jobs:
                    k = fm_tile(lambda c, wc=wc: win[:, c, wc:wc + 128], hsl)
                    o = cnt["fo"] % 4
                    cnt["fo"] += 1
                    eng = "act" if cnt["fo"] % 2 == 0 else "dve"
                    if eng == "act":
                        fn = lambda e, o=o, k=k: e.activation(out=fo[o][:], in_=pf[k][:], func=AF.Copy)
                    else:
                        fn = lambda e, o=o, k=k: e.tensor_copy(out=fo[o][:], in_=pf[k][:])
                    S.add(eng, fn, reads=[("pf", k)], writes=[("fo", o)])
                    S.add("sp", lambda e, o=o, row0=row0, tb=tb: e.dma_start(out=qk2[row0:row0 + 128, tb:tb + ST], in_=fo[o][:]),
                          reads=[("fo", o)], writes=[("qk", row0, tb)], dma=True)
                if own is not None:
                    for i in range(16):
                        wc = COL["G"] + 128 * i
                        k = fm_tile(lambda c, wc=wc: win[:, c, wc:wc + 128], hsl)
                        o = cnt["fo"] % 4
                        cnt["fo"] += 1
                        S.add("act", lambda e, o=o, k=k, i=i: e.activation(out=fo[o][:], in_=pf[k][:], func=AF.Sigmoid,
                                                                        bias=bg[:, i:i + 1]),
                              reads=[("pf", k), "bg"], writes=[("fo", o)])
                        S.add("sp", lambda e, o=o, i=i, own=own: e.dma_start(out=gT[i * 128:(i + 1) * 128, own:own + ST], in_=fo[o][:]),
                              reads=[("fo", o)], writes=[("gT", i, own)], dma=True)
                csl = cnt["cs"] % 2
                cnt["cs"] += 1
                S.add("sp", lambda e, csl=csl, tb=tb: e.dma_start(out=cs[csl][:], in_=rope[:, :, tb:tb + ST].rearrange("a p t -> p a t")),
                      writes=[("cs", csl)], dma=True)
                for rc in range(3 if own is not None else 2):
                    k1 = fm_tile(lambda c, rc=rc: wr[:, c, rc * 128:(rc + 1) * 128], hsl)
                    k2 = fm_tile(lambda c, rc=rc: wrp[:, c, rc * 128:(rc + 1) * 128], hsl)
                    o = cnt["ro"] % 2
                    cnt["ro"] += 1
                    S.add("dve", lambda e, k1=k1, csl=csl: e.tensor_tensor(out=r1[:], in0=pf[k1][:], in1=cs[csl][:, 0, :], op=ALU.mult),
                          reads=[("pf", k1), ("cs", csl)], writes=["r1"])
                    S.add("dve", lambda e, k2=k2, csl=csl: e.tensor_tensor(out=r2[:], in0=pf[k2][:], in1=cs[csl][:, 1, :], op=ALU.mult),
                          reads=[("pf", k2), ("cs", csl)], writes=["r2"])
                    S.add("dve", lambda e, o=o: e.tensor_tensor(out=ro[o][:], in0=r1[:], in1=r2[:], op=ALU.add),
                          reads=["r1", "r2"], writes=[("ro", o)])
                    for h8 in range(8):
                        rh = rc * 8 + h8
                        if rh >= 12 and own is None:
                            continue
                        head = (HK_A + rh) if rh < 12 else (HQ_A + rh - 12)
                        wkeys = [("qk", (head // 2) * 128, tb)]
                        S.add("sp", lambda e, o=o, h8=h8, head=head, tb=tb: e.dma_start(
                            out=qk2[head * 64:head * 64 + 16, tb:tb + ST], in_=ro[o][h8 * 16:(h8 + 1) * 16, :]),
                            reads=[("ro", o)], writes=wkeys, dma=True)
            for g, d in enumerate((1, 4, 16)):
                for lt in range(16):
                    if d == 1:
                        sel = lambda c, lt=lt: hT[:, c, lt * 128:(lt + 1) * 128]
                    elif d == 4:
                        s4, r = lt // 4, lt % 4
                        sel = lambda c, s4=s4, r=r: hT[:, c, s4 * 512 + r:s4 * 512 + 512:4]
                    else:
                        sel = lambda c, lt=lt: hT[:, c, lt:MT:16]
                    k = cnt["v"] % 2
                    cnt["v"] += 1
                    wc = COL["VA"] + 256 * g
                    S.add("pe", [lambda e, c=c, k=k, sel=sel, wc=wc: e.matmul(pv[k][:, 0:256], lhsT=sel(c), rhs=win[:, c, wc:wc + 256],
                                                                            start=(c == 0), stop=(c == 7)) for c in range(8)],
                          reads=["hT"], writes=[("pv", k)])
                    if lt % 2 == 0:
                        S.add("act", lambda e, k=k: e.activation(out=vo[k][:, 0:256], in_=pv[k][:, 0:256], func=AF.Copy),
                              reads=[("pv", k)], writes=[("vo", k)])
                    else:
                        S.add("dve", lambda e, k=k: e.tensor_copy(out=vo[k][:, 0:256], in_=pv[k][:, 0:256]),
                              reads=[("pv", k)], writes=[("vo", k)])
                    tid = mt * 16 + lt
                    S.add("sp", lambda e, k=k, g=g, tid=tid: e.dma_start(out=vtA[g, tid, :, :], in_=vo[k][:, 0:256]),
                          reads=[("vo", k)], writes=[("vtA", g, tid)], dma=True)
            for lt in range(16):
                k = cnt["v"] % 2
                cnt["v"] += 1
                wc = COL["VB"]
                S.add("pe", [lambda e, c=c, k=k, lt=lt, wc=wc: e.matmul(pv[k][:], lhsT=hT[:, c, lt * 128:(lt + 1) * 128],
                                                                       rhs=win[:, c, wc:wc + 512], start=(c == 0), stop=(c == 7))
                             for c in range(8)], reads=["hT"], writes=[("pv", k)])
                S.add("dve", lambda e, k=k: e.tensor_copy(out=vo[k][:], in_=pv[k][:]), reads=[("pv", k)], writes=[("vo", k)])
                tid = mt * 16 + lt
                S.add("sp", lambda e, k=k, tid=tid: e.dma_start(out=vtB[tid, :, :], in_=vo[k][:]),
                      reads=[("vo", k)], writes=[("vtB", tid)], dma=True)
        S.flush()


def p3_phase(nc, S, qkT, vtA, vexp, bmA, oT):
    NB = 2
    with ExitStack() as es:
        EXTM = max(PIECES) + 2 * HALO
        KT = [es.enter_context(nc.sbuf_tensor("aKT%d" % i, [64, EXTM], BF16)) for i in range(2)]
        QT = [es.enter_context(nc.sbuf_tensor("aQT%d" % i, [64, EXTM], BF16)) for i in range(2)]
        Vt = [es.enter_context(nc.sbuf_tensor("aVt%d" % i, [128, EXTM // 128, 128], BF16)) for i in range(2)]
        acc = es.enter_context(nc.sbuf_tensor("aacc", [128, EXTM], F32))
        bm = es.enter_context(nc.sbuf_tensor("abm", [128, 2 * NB, 128], BF16))
        NSL = 4
        pT = [es.enter_context(nc.sbuf_tensor("apT%d" % i, [128, 2 * NB, 128], BF16)) for i in range(NSL)]
        FC = 1024
        rz = es.enter_context(nc.sbuf_tensor("arz", [64, FC], F32))
        oo = [es.enter_context(nc.sbuf_tensor("aoo%d" % i, [64, FC], BF16)) for i in range(2)]
        Vall = [es.enter_context(nc.sbuf_tensor("aVall%d" % i, [128, EXTM // 128, 256], BF16)) for i in range(3)]
        vx = [es.enter_context(nc.sbuf_tensor("avx%d" % i, [128, EXTM // 128, 64], BF16)) for i in range(3)]
        PS = es.enter_context(nc.psum_tensor("aPS", [128, 32, 128], F32))
        for i in range(2):
            S.add("dve", lambda e, i=i: e.memset(QT[i][:], 0.0), writes=[("QT", i)])
        for b in range(NB):
            S.add("sp", lambda e, b=b: e.dma_start(out=bm[:, 2 * b:2 * b + 2, :], in_=bmA.rearrange("p (t q) -> p t q", t=2)), writes=["bm"], dma=True)
        jn = 0
        ld = 0
        oc = 0
        pend = []
        LAG = 2
        its3 = [(p, n, hg, g, d) for p, n in enumerate(PIECES) for hg in range(4) for g, d in enumerate((1, 4, 16))]

        def loads3(it):
            p, n, hg, g, d = its3[it]
            ext = n + 2 * HALO
            nt = ext // 128
            t00 = EOFF[p] // 128
            sl = it % 2
            if hg == 0 and g == 0:
                for gg in range(3):
                    for a in range(0, nt, 8):
                        S.add("sp", lambda e, gg=gg, a=a, t00=t00: e.dma_start(
                            out=Vall[gg][:, a:a + 8, :], in_=vtA[gg, t00 + a:t00 + a + 8, :, :].rearrange("t p c -> p t c")),
                            reads=[("vt",)], writes=[("Vall", gg)], dma=True)
                    S.add("sp", lambda e, gg=gg, t00=t00, nt=nt: e.dma_start(out=vx[gg][:, 0:nt, :], in_=vexp[:, gg, t00:t00 + nt, :]),
                          writes=[("vx", gg)], dma=True)
            hk, hq = HK_A + 4 * g + hg, HQ_A + 4 * g + hg
            S.add("sp", lambda e, sl=sl, hk=hk, p=p, ext=ext: e.dma_start(
                out=KT[sl][:, 0:ext], in_=qkT[hk * 64:hk * 64 + 64, EOFF[p]:EOFF[p] + ext]),
                reads=[("qk",)], writes=[("KT", sl)], dma=True)
            S.add("sp", lambda e, sl=sl, hq=hq, p=p, n=n: e.dma_start(
                out=QT[sl][:, HALO:HALO + n], in_=qkT[hq * 64:hq * 64 + 64, EOFF[p] + HALO:EOFF[p] + HALO + n]),
                reads=[("qk",)], writes=[("QT", sl)], dma=True)
            S.add("pool", lambda e, sl=sl, g=g, hg=hg, nt=nt: e.tensor_copy(
                out=Vt[sl][:, 0:nt, 0:64], in_=Vall[g][:, 0:nt, hg * 64:(hg + 1) * 64]),
                reads=[("Vall", g)], writes=[("Vt", sl, 0)])
            S.add("act", lambda e, sl=sl, g=g, nt=nt: e.activation(
                out=Vt[sl][:, 0:nt, 64:128], in_=vx[g][:, 0:nt, :], func=AF.Copy),
                reads=[("vx", g)], writes=[("Vt", sl, 1)])

        loads3(0)
        for it, (p, n, hg, g, d) in enumerate(its3):
            if True:
                if True:
                    ext = n + 2 * HALO
                    nt = ext // 128
                    t00 = EOFF[p] // 128
                    sl = it % 2
                    if g == 0:
                        S.add("dve", lambda e: e.memset(acc[:], 0.0), writes=[("acc", b) for b in range(EXTM // 1024)])
                    i_lo = (HALO // d - 64) // 128
                    i_hi = -(-((HALO + n) // d - 64) // 128) - 1
                    tiles = [(r, i) for r in range(d) for i in range(i_lo, i_hi + 1)]
                    for b0 in range(0, len(tiles), NB):
                        if b0 == NB * (LAG + 1) and it + 1 < len(its3):
                            loads3(it + 1)
                        bt = tiles[b0:b0 + NB]
                        nb = len(bt)
                        k = jn % NSL
                        jn += 1
                        psk = lambda bi, t, k=k: PS[:, 8 * k + 2 * bi + t, :]
                        pok = lambda bi, k=k: PS[:, 8 * k + 4 + bi, :]
                        fns = []
                        for bi, (r, i) in enumerate(bt):
                            q0 = d * (128 * i + 64) + r
                            for t in range(2):
                                k0 = d * 128 * (i + t) + r
                                fns.append(lambda e, bi=bi, t=t, k0=k0, q0=q0, sl=sl, d=d, psk=psk: e.matmul(
                                    psk(bi, t), lhsT=KT[sl][:, k0:k0 + 127 * d + 1:d], rhs=QT[sl][:, q0:q0 + 127 * d + 1:d],
                                    start=True, stop=True))
                        S.add("pe", fns, reads=[("KT", sl), ("QT", sl)], writes=[("ps", k)])
                        S.add("act", lambda e, k=k, nb=nb: e.activation(
                            out=pT[k][:, 0:2 * nb, :], in_=PS[:, 8 * k:8 * k + 2 * nb, :], func=AF.Exp, scale=0.125),
                            reads=[("ps", k)], writes=[("pT", k)])
                        S.add("dve", lambda e, k=k, nb=nb: e.tensor_tensor(out=pT[k][:, 0:2 * nb, :], in0=pT[k][:, 0:2 * nb, :],
                                                                          in1=bm[:, 0:2 * nb, :], op=ALU.mult),
                              reads=[("pT", k), "bm"], writes=[("pT", k)])
                        def pv(bt=bt, nb=nb, k=k, sl=sl, d=d, g=g, pok=pok):
                            fns = []
                            for bi, (r, i) in enumerate(bt):
                                for t in range(2):
                                    fns.append(lambda e, bi=bi, t=t, k=k, sl=sl, i=i, d=d, r=r, pok=pok: e.matmul(
                                        pok(bi), lhsT=Vt[sl][:, (i + t) * d + r, :], rhs=pT[k][:, 2 * bi + t, :],
                                        start=(t == 0), stop=(t == 1)))
                            S.add("pe", fns, reads=[("pT", k), ("Vt", sl, 0), ("Vt", sl, 1)], writes=[("po", k)])
                            merged = (nb == 2 and bt[0][0] == bt[1][0] and bt[1][1] == bt[0][1] + 1)
                            if merged:
                                r, i = bt[0]
                                q0 = d * (128 * i + 64) + r
                                evs = [(acc[:, q0:q0 + 255 * d + 1:d], PS[:, 8 * k + 4:8 * k + 6, :].rearrange("p a b -> p (a b)"),
                                        [("acc", b) for b in range(q0 // 1024, (q0 + 255 * d) // 1024 + 1)])]
                            else:
                                evs = []
                                for bi, (r, i) in enumerate(bt):
                                    q0 = d * (128 * i + 64) + r
                                    evs.append((acc[:, q0:q0 + 127 * d + 1:d], pok(bi),
                                                [("acc", b) for b in range(q0 // 1024, (q0 + 127 * d) // 1024 + 1)]))
                            for (asl, src, akeys) in evs:
                                if g == 0:
                                    S.add("dve", lambda e, asl=asl, src=src: e.tensor_copy(out=asl, in_=src),
                                          reads=[("po", k)], writes=akeys)
                                else:
                                    S.add("dve", lambda e, asl=asl, src=src: e.tensor_tensor(out=asl, in0=src, in1=asl, op=ALU.add),
                                          reads=[("po", k)] + akeys, writes=akeys)

                        pend.append(pv)
                        if len(pend) > LAG:
                            pend.pop(0)()
                if g == 2:
                    while pend:
                        pend.pop(0)()
                    for c0 in range(0, n, FC):
                        o = oc % 2
                        oc += 1
                        S.add("act", lambda e, c0=c0: e.activation(out=rz[:], in_=acc[64:128, HALO + c0:HALO + c0 + FC], func=AF.Ln),
                              reads=[("acc", (HALO + c0) // 1024)], writes=["rz"])
                        S.add("act", lambda e: e.activation(out=rz[:], in_=rz[:], func=AF.Exp, scale=-1.0),
                              reads=["rz"], writes=["rz"])
                        S.add("dve", lambda e, c0=c0: e.tensor_tensor(out=rz[:], in0=acc[0:64, HALO + c0:HALO + c0 + FC], in1=rz[:], op=ALU.mult),
                              reads=[("acc", (HALO + c0) // 1024), "rz"], writes=["rz"])
                        S.add("act", lambda e, o=o: e.activation(out=oo[o][:], in_=rz[:], func=AF.Copy),
                              reads=["rz"], writes=[("oo", o)])
                        S.add("sp", lambda e, o=o, hg=hg, p=p, c0=c0: e.dma_start(
                            out=oT[hg * 64:(hg + 1) * 64, OOFF[p] + c0:OOFF[p] + c0 + FC], in_=oo[o][:]),
                            reads=[("oo", o)], writes=[("oT", hg, p, c0)], dma=True)
        S.flush()


JT_TILES = ((2, 5), (2, 6), (2, 5), (2, 5), (1, 6))
JT_OFF = (0, 5, 11, 16, 21)


def p4_phase(nc, S, qkT, vtB, bias4, mB, oT):
    with ExitStack() as es:
        EXTM = max(PIECES) + 2 * HALO
        KT = [es.enter_context(nc.sbuf_tensor("bKT%d" % i, [64, EXTM], BF16)) for i in range(2)]
        QT = [es.enter_context(nc.sbuf_tensor("bQT%d" % i, [64, EXTM], BF16)) for i in range(2)]
        Vt = [es.enter_context(nc.sbuf_tensor("bVt%d" % i, [128, EXTM // 128, 128], BF16)) for i in range(2)]
        VallB = es.enter_context(nc.sbuf_tensor("bVall", [128, EXTM // 128, 512], BF16))
        b4 = es.enter_context(nc.sbuf_tensor("bb4", [128, 8, 128], F32))
        eb = es.enter_context(nc.sbuf_tensor("beb", [128, 8, 128], F32))
        mb = es.enter_context(nc.sbuf_tensor("bmb", [128, 27, 128], BF16))
        W = [es.enter_context(nc.sbuf_tensor("bW%d" % i, [128, 27, 128], BF16)) for i in range(2)]
        NSL = 4
        pT = [es.enter_context(nc.sbuf_tensor("bpT%d" % i, [128, 6, 128], BF16)) for i in range(NSL)]
        rz = [es.enter_context(nc.sbuf_tensor("brz%d" % i, [64, 512], F32)) for i in range(2)]
        oo = [es.enter_context(nc.sbuf_tensor("boo%d" % i, [64, EXTM - 2 * HALO], BF16)) for i in range(2)]
        PS = es.enter_context(nc.psum_tensor("bPS", [128, 32, 128], F32))
        S.add("sp", lambda e: e.dma_start(out=mb[:], in_=mB), writes=["mb"], dma=True)
        for i in range(2):
            S.add("pool", lambda e, i=i: e.memset(Vt[i][:, :, 64:128], 1.0), writes=[("Vt", i)])
        jn = 0
        gn = 0
        ld = 0
        pend = []
        LAG = 2
        its = [(p, n, h) for p, n in enumerate(PIECES) for h in range(8)]

        def emit_loads(it):
            p, n, h = its[it]
            ext = n + 2 * HALO
            nt = ext // 128
            t00 = EOFF[p] // 128
            sl = it % 2
            wsl = h % 2
            if h == 0:
                for a in range(0, nt, 8):
                    S.add("sp", lambda e, a=a, t00=t00: e.dma_start(
                        out=VallB[:, a:a + 8, :], in_=vtB[t00 + a:t00 + a + 8, :, :].rearrange("t p c -> p t c")),
                        reads=[("vt",)], writes=["VallB"], dma=True)
            S.add("sp", lambda e, h=h: e.dma_start(out=b4[:], in_=bias4[h].rearrange("b p q -> p b q")), writes=["b4"], dma=True)
            hk, hq = HK_B + h, HQ_B + h
            S.add("sp", lambda e, sl=sl, hk=hk, p=p, ext=ext: e.dma_start(
                out=KT[sl][:, 0:ext], in_=qkT[hk * 64:hk * 64 + 64, EOFF[p]:EOFF[p] + ext]),
                reads=[("qk",)], writes=[("KT", sl)], dma=True)
            S.add("sp", lambda e, sl=sl, hq=hq, p=p, ext=ext: e.dma_start(
                out=QT[sl][:, 0:ext], in_=qkT[hq * 64:hq * 64 + 64, EOFF[p]:EOFF[p] + ext]),
                reads=[("qk",)], writes=[("QT", sl)], dma=True)
            S.add("pool", lambda e, sl=sl, h=h, nt=nt: e.tensor_copy(
                out=Vt[sl][:, 0:nt, 0:64], in_=VallB[:, 0:nt, h * 64:(h + 1) * 64]),
                reads=["VallB"], writes=[("Vt", sl)])

        def emit_tables(it):
            p, n, h = its[it]
            wsl = h % 2
            S.add("act", lambda e: e.activation(out=eb[:], in_=b4[:], func=AF.Exp), reads=["b4"], writes=["eb"])
            for jt in range(5):
                b0, ntl = JT_TILES[jt]
                S.add("dve", lambda e, jt=jt, b0=b0, ntl=ntl, wsl=wsl: e.tensor_tensor(
                    out=W[wsl][:, JT_OFF[jt]:JT_OFF[jt] + ntl, :], in0=eb[:, b0:b0 + ntl, :],
                    in1=mb[:, JT_OFF[jt]:JT_OFF[jt] + ntl, :], op=ALU.mult),
                    reads=["eb", "mb"], writes=[("W", wsl)])

        emit_loads(0)
        emit_tables(0)
        for it, (p, n, h) in enumerate(its):
            if it + 1 < len(its):
                emit_loads(it + 1)
            if True:
                wsl = h % 2
                sl = it % 2
                nrp = n // 128
                osl = h % 2
                for rp in range(nrp):
                    jt = 1 if rp == 0 else 2 if rp == 1 else 4 if rp == nrp - 1 else 3 if rp == nrp - 2 else 0
                    b0, ntl = JT_TILES[jt]
                    dlt0 = -8 + 2 * b0
                    if rp == 8 and it + 1 < len(its):
                        emit_tables(it + 1)
                    k = jn % NSL
                    jn += 1
                    gk = k
                    gi = 0
                    poq = 8 * k + 6
                    q0 = HALO + 128 * rp
                    S.add("pe", [lambda e, k=k, t=t, sl=sl, q0=q0, dlt0=dlt0: e.matmul(
                        PS[:, 8 * k + t, :], lhsT=KT[sl][:, q0 + 64 * (dlt0 + 2 * t):q0 + 64 * (dlt0 + 2 * t) + 128],
                        rhs=QT[sl][:, q0:q0 + 128], start=True, stop=True) for t in range(ntl)],
                        reads=[("KT", sl), ("QT", sl)], writes=[("ps", k), ("po", k, 0)])
                    S.add("act", lambda e, k=k, ntl=ntl: e.activation(out=pT[k][:, 0:ntl, :], in_=PS[:, 8 * k:8 * k + ntl, :],
                                                                     func=AF.Exp, scale=0.125),
                          reads=[("ps", k)], writes=[("pT", k)])
                    S.add("dve", lambda e, k=k, ntl=ntl, jt=jt, wsl=wsl: e.tensor_tensor(
                        out=pT[k][:, 0:ntl, :], in0=pT[k][:, 0:ntl, :], in1=W[wsl][:, JT_OFF[jt]:JT_OFF[jt] + ntl, :],
                        op=ALU.mult), reads=[("pT", k), ("W", wsl)], writes=[("pT", k)])
                    def pv(k=k, sl=sl, q0=q0, dlt0=dlt0, ntl=ntl, poq=poq, gk=gk, gi=gi, rk=jn % 2, osl=osl, rp=rp):
                        S.add("pe", [lambda e, k=k, t=t, sl=sl, q0=q0, dlt0=dlt0, ntl=ntl, poq=poq: e.matmul(
                            PS[:, poq, :], lhsT=Vt[sl][:, (q0 + 64 * (dlt0 + 2 * t)) // 128, :], rhs=pT[k][:, t, :],
                            start=(t == 0), stop=(t == ntl - 1)) for t in range(ntl)],
                            reads=[("pT", k), ("Vt", sl)], writes=[("po", gk, gi)])
                        S.add("act", lambda e, rk=rk, poq=poq: e.activation(out=rz[rk][:, 0:128], in_=PS[64:128, poq, :], func=AF.Ln),
                              reads=[("po", gk, gi)], writes=[("rz", rk)])
                        S.add("act", lambda e, rk=rk: e.activation(out=rz[rk][:, 0:128], in_=rz[rk][:, 0:128], func=AF.Exp, scale=-1.0),
                              reads=[("rz", rk)], writes=[("rz", rk)])
                        S.add("dve", lambda e, rk=rk, osl=osl, rp=rp, poq=poq: e.tensor_tensor(
                            out=oo[osl][:, rp * 128:(rp + 1) * 128], in0=PS[0:64, poq, :], in1=rz[rk][:, 0:128], op=ALU.mult),
                            reads=[("po", gk, gi), ("rz", rk)], writes=[("oo", osl, rp)])

                    pend.append(pv)
                    if len(pend) > LAG:
                        pend.pop(0)()
                while pend:
                    pend.pop(0)()
                S.add("sp", lambda e, osl=osl, h=h, p=p, n=n: e.dma_start(
                    out=oT[256 + h * 64:256 + (h + 1) * 64, OOFF[p]:OOFF[p] + n], in_=oo[osl][:, 0:n]),
                    reads=[("oo", osl, i) for i in range(nrp)], writes=[("oT", 4 + h, p)], dma=True)
        S.flush()


def p5a_phase(nc, S, oT, gT, x1, w_a, w_b, w_o, post_g, x2):
    with ExitStack() as es:
        wa = es.enter_context(nc.sbuf_tensor("cwa", [128, 2, D], BF16))
        wb = es.enter_context(nc.sbuf_tensor("cwb", [128, 4, D], BF16))
        wo = es.enter_context(nc.sbuf_tensor("cwo", [128, 8, D], BF16))
        gpost = es.enter_context(nc.sbuf_tensor("cgpost", [128, D], F32))
        epsb = es.enter_context(nc.sbuf_tensor("cepsb", [128, 1], F32))
        with ExitStack() as es2:
            stg = [es2.enter_context(nc.sbuf_tensor("cstg%d" % i, [128, D], F32)) for i in range(3)]
            S.add("sp", lambda e: e.dma_start(out=gpost[:], in_=bcast_rows(post_g)), writes=["gpost"], dma=True)
            S.add("pool", lambda e: e.memset(epsb[:], EPS), writes=["epsb"])
            k = 0
            for (src, dst, nch) in ((w_a, wa, 2), (w_b, wb, 4), (w_o, wo, 8)):
                for c in range(nch):
                    sl = k % 3
                    S.add("sp", lambda e, src=src, c=c, sl=sl: e.dma_start(out=stg[sl][:], in_=src[c * 128:(c + 1) * 128, :]),
                          writes=[("stg", sl)], dma=True)
                    S.add("dve" if k % 2 else "act",
                          (lambda e, dst=dst, c=c, sl=sl: e.tensor_copy(out=dst[:, c, :], in_=stg[sl][:])) if k % 2 else
                          (lambda e, dst=dst, c=c, sl=sl: e.activation(out=dst[:, c, :], in_=stg[sl][:], func=AF.Copy)),
                          reads=[("stg", sl)], writes=[("w", id(dst), c)])
                    k += 1
            S.flush()
        os_ = [es.enter_context(nc.sbuf_tensor("cos%d" % i, [128, 6, ST], BF16)) for i in range(2)]
        gs = [es.enter_context(nc.sbuf_tensor("cgs%d" % i, [128, 16, ST], BF16)) for i in range(2)]
        uT = [es.enter_context(nc.sbuf_tensor("cuT%d" % i, [128, 8, ST], BF16)) for i in range(2)]
        t1 = [es.enter_context(nc.sbuf_tensor("ct1%d" % i, [128, ST], F32)) for i in range(2)]
        t2 = [es.enter_context(nc.sbuf_tensor("ct2%d" % i, [128, ST], F32)) for i in range(2)]
        xr = [es.enter_context(nc.sbuf_tensor("cxr%d" % i, [128, D], F32)) for i in range(3)]
        tts = [es.enter_context(nc.sbuf_tensor("ctt%d" % i, [128, D], F32)) for i in range(3)]
        junk = es.enter_context(nc.sbuf_tensor("cjunk", [128, D], BF16))
        sty = es.enter_context(nc.sbuf_tensor("csty", [128, 8], F32))
        pa = [es.enter_context(nc.psum_tensor("cpa%d" % i, [128, ST], F32)) for i in range(2)]
        pb = [es.enter_context(nc.psum_tensor("cpb%d" % i, [128, ST], F32)) for i in range(2)]
        py = [es.enter_context(nc.psum_tensor("cpy%d" % i, [128, D], F32)) for i in range(2)]
        S.add("dve", lambda e: e.tensor_scalar(out=gpost[:], in0=gpost[:], scalar1=1.0, scalar2=None, op0=ALU.mult),
              reads=["gpost"], writes=["gpost"])
        def stage1(s):
            sl = s % 2
            tb = s * ST
            S.add("sp", lambda e, sl=sl, tb=tb: e.dma_start(out=os_[sl][:], in_=oT[:, tb:tb + ST].rearrange("(c p) t -> p c t", p=128)),
                  reads=[("oT",)], writes=[("os", sl)], dma=True)
            S.add("sp", lambda e, sl=sl, tb=tb: e.dma_start(out=gs[sl][:], in_=gT[:, tb:tb + ST].rearrange("(c p) t -> p c t", p=128)),
                  reads=[("gT",)], writes=[("gs", sl)], dma=True)
            for dc in range(8):
                k = dc % 2
                S.add("pe", [lambda e, k=k, cc=cc, dc=dc, sl=sl: e.matmul(pa[k][:], lhsT=wa[:, cc, dc * 128:(dc + 1) * 128],
                                                                        rhs=os_[sl][:, cc, :], start=(cc == 0), stop=(cc == 1))
                             for cc in range(2)], reads=[("os", sl)], writes=[("pa", k)])
                S.add("pe", [lambda e, k=k, cc=cc, dc=dc, sl=sl: e.matmul(pb[k][:], lhsT=wb[:, cc, dc * 128:(dc + 1) * 128],
                                                                        rhs=os_[sl][:, 2 + cc, :], start=(cc == 0), stop=(cc == 3))
                             for cc in range(4)], reads=[("os", sl)], writes=[("pb", k)])
                S.add("dve", lambda e, k=k, dc=dc, sl=sl: e.tensor_tensor(out=t1[k][:], in0=pa[k][:], in1=gs[sl][:, dc, :], op=ALU.mult),
                      reads=[("pa", k), ("gs", sl)], writes=[("t1", k)])
                S.add("dve", lambda e, k=k, dc=dc, sl=sl: e.tensor_tensor(out=t2[k][:], in0=pb[k][:], in1=gs[sl][:, 8 + dc, :], op=ALU.mult),
                      reads=[("pb", k), ("gs", sl)], writes=[("t2", k)])
                S.add("pool", lambda e, dc=dc, sl=sl, k=k: e.tensor_tensor(out=uT[sl][:, dc, :], in0=t1[k][:], in1=t2[k][:], op=ALU.add),
                      reads=[("t1", k), ("t2", k)], writes=[("uT", sl)])

        def chain2(jj):
            xs = jj % 3
            ts = jj % 3
            c0 = (jj % 4) * 2
            S.add("act", lambda e, c0=c0, ts=ts: e.activation(out=junk[:], in_=tts[ts][:], func=AF.Square,
                                                             accum_out=sty[:, c0:c0 + 1]),
                  reads=[("tts", ts)], writes=["junk", ("sty", c0)])
            S.add("act", lambda e, c0=c0: e.activation(out=sty[:, c0 + 1:c0 + 2], in_=sty[:, c0:c0 + 1],
                                                     func=AF.Sqrt, scale=1.0 / D, bias=epsb[:, 0:1]),
                  reads=[("sty", c0), "epsb"], writes=[("sty", c0 + 1)])
            S.add("dve", lambda e, c0=c0: e.reciprocal(out=sty[:, c0 + 1:c0 + 2], in_=sty[:, c0 + 1:c0 + 2]),
                  reads=[("sty", c0 + 1)], writes=[("sty", c0 + 1)])
            S.add("dve", lambda e, c0=c0, ts=ts: e.scalar_tensor_tensor(out=tts[ts][:], in0=tts[ts][:], scalar=sty[:, c0 + 1:c0 + 2],
                                                                      in1=gpost[:], op0=ALU.mult, op1=ALU.mult),
                  reads=[("tts", ts), ("sty", c0 + 1), "gpost"], writes=[("tts", ts)])
            S.add("pool", lambda e, xs=xs, ts=ts: e.tensor_tensor(out=xr[xs][:], in0=tts[ts][:], in1=xr[xs][:], op=ALU.add),
                  reads=[("tts", ts), ("xr", xs)], writes=[("xr", xs)])
            S.add("sp", lambda e, xs=xs, jj=jj: e.dma_start(out=x2[jj * 128:(jj + 1) * 128, :], in_=xr[xs][:]),
                  reads=[("xr", xs)], writes=[("x2", jj)], dma=True)

        def stage2(s):
            sl = s % 2
            for j in range(4):
                jj = s * 4 + j
                xs = jj % 3
                ps_ = jj % 2
                S.add("sp", lambda e, jj=jj, xs=xs: e.dma_start(out=xr[xs][:], in_=x1[jj * 128:(jj + 1) * 128, :]),
                      reads=[("x1",)], writes=[("xr", xs)], dma=True)
                for half in range(2):
                    S.add("pe", [lambda e, c=c, j=j, half=half, ps_=ps_, sl=sl: e.matmul(
                        py[ps_][:, half * 512:(half + 1) * 512], lhsT=uT[sl][:, c, j * 128:(j + 1) * 128],
                        rhs=wo[:, c, half * 512:(half + 1) * 512], start=(c == 0), stop=(c == 7)) for c in range(8)],
                        reads=[("uT", sl)], writes=[("py", ps_, half)])
                S.add("dve", lambda e, ps_=ps_, xs=xs: e.tensor_copy(out=tts[xs][:], in_=py[ps_][:]),
                      reads=[("py", ps_, 0), ("py", ps_, 1)], writes=[("tts", xs)])
                if jj > 0:
                    chain2(jj - 1)

        nblk = NO // ST
        stage1(0)
        for s in range(nblk):
            if s + 1 < nblk:
                stage1(s + 1)
            stage2(s)
        chain2(NO // 128 - 1)
        S.flush()


def build_program(debug=None, force_internal=False):
    nc = bass.Bass("TRN2", target_bir_lowering=False)

    def din(name, shape, dt=F32):
        return nc.dram_tensor(name, list(shape), dt, kind="ExternalInput").ap()

    def dscr(name, shape, dt, out=False):
        return nc.dram_tensor(name, list(shape), dt, kind=("ExternalOutput" if out else "Internal")).ap()

    dbg = (debug is not None) and not force_internal
    xe = din("xe", [TE, D])
    f1g = din("ffn1_w_gate", [D, DFF]); f1u = din("ffn1_w_up", [D, DFF]); f1d = din("ffn1_w_down", [DFF, D])
    f1pre = din("ffn1_pre_g", [D]); f1post = din("ffn1_post_g", [D])
    f2g = din("ffn2_w_gate", [D, DFF]); f2u = din("ffn2_w_up", [D, DFF]); f2d = din("ffn2_w_down", [DFF, D])
    f2pre = din("ffn2_pre_g", [D]); f2post = din("ffn2_post_g", [D])
    mixpre = din("mix_pre_g", [D]); mixpost = din("mix_post_g", [D])
    w_in = din("w_in", [D, INW]); b_gate = din("b_gate", [2048])
    w_a = din("w_branch_a", [256, D]); w_b = din("w_branch_b", [512, D]); w_o = din("w_out", [D, D])
    rope = din("rope", [2, 128, TE])
    vexp = din("vexp", [128, 3, TE // 128, 64], BF16)
    bmA = din("bmA", [128, 256], BF16)
    bias4 = din("bias4", [8, 8, 128, 128])
    mB = din("mB", [128, 27, 128], BF16)
    x1 = dscr("x1", [NO, D], F32, out=dbg)
    h2 = dscr("h2", [TE, D], BF16, out=dbg)
    qkT = dscr("qkT", [40 * 64, TE], BF16, out=dbg)
    vtA = dscr("vtA", [3, TE // 128, 128, 256], BF16, out=dbg)
    vtB = dscr("vtB", [TE // 128, 128, 512], BF16, out=dbg)
    gT = dscr("gT", [2048, NO], BF16, out=dbg)
    oT = dscr("oT", [768, NO], BF16, out=dbg)
    x2 = dscr("x2", [NO, D], F32, out=dbg)
    yout = dscr("yout", [NO, D], F32, out=True)

    with ExitStack() as es:
        S = Sched(nc, es)

        def out_p1(S, jj, xt, key, bufs):
            t0 = jj * 128
            own = own_block(t0 - t0 % ST)
            if own is not None:
                own += t0 % ST
                S.add("sp", lambda e, own=own: e.dma_start(out=x1[own:own + 128, :], in_=xt[:]),
                      reads=[key], writes=[("x1", own)], dma=True)
            ob, st_o, hbb, epsb = bufs["ob"], bufs["st_o"], bufs["hb"], bufs["epsb"]
            junk = hbb[jj % 4]
            sl = jj % 2
            c0 = (jj % 4) * 2
            S.add("act", lambda e, c0=c0: e.activation(out=junk[:], in_=xt[:], func=AF.Square,
                                                     accum_out=st_o[:, c0:c0 + 1]),
                  reads=[key], writes=[("hb", jj % 4), ("sto", c0)])
            S.add("act", lambda e, c0=c0: e.activation(out=st_o[:, c0 + 1:c0 + 2], in_=st_o[:, c0:c0 + 1],
                                                     func=AF.Sqrt, scale=1.0 / D, bias=epsb[:, 0:1]),
                  reads=[("sto", c0)], writes=[("sto", c0 + 1)])
            S.add("dve", lambda e, c0=c0: e.reciprocal(out=st_o[:, c0 + 1:c0 + 2], in_=st_o[:, c0 + 1:c0 + 2]),
                  reads=[("sto", c0 + 1)], writes=[("sto", c0 + 1)])
            S.add("act", lambda e, c0=c0, sl=sl: e.activation(out=ob[sl][:], in_=xt[:], func=AF.Copy,
                                                             scale=st_o[:, c0 + 1:c0 + 2]),
                  reads=[key, ("sto", c0 + 1)], writes=[("ob", sl)])
            S.add("sp", lambda e, sl=sl, t0=t0: e.dma_start(out=h2[t0:t0 + 128, :], in_=ob[sl][:]),
                  reads=[("ob", sl)], writes=[("h2", t0)], dma=True)

        def out_p5(S, jj, xt, key, bufs):
            S.add("sp", lambda e, jj=jj: e.dma_start(out=yout[jj * 128:(jj + 1) * 128, :], in_=xt[:]),
                  reads=[key], writes=[("yout", jj)], dma=True)

        ph = debug or "all"
        if ph in ("all", "p1", "p1s"):
            ffn_phase(nc, S, "f1", xe, TE if ph != "p1s" else 1024, f1g, f1u, f1d, f1pre, f1post, out_p1)
        if ph in ("all", "p2"):
            p2_phase(nc, S, h2, w_in, mixpre, b_gate, rope, qkT, vtA, vtB, gT, TE // MT)
        if ph in ("all", "p3"):
            p3_phase(nc, S, qkT, vtA, vexp, bmA, oT)
        if ph in ("all", "p4"):
            p4_phase(nc, S, qkT, vtB, bias4, mB, oT)
        if ph in ("all", "p5"):
            p5a_phase(nc, S, oT, gT, x1, w_a, w_b, w_o, mixpost, x2)
            ffn_phase(nc, S, "f2", x2, NO, f2g, f2u, f2d, f2pre, f2post, out_p5)
        sched_finish(S)
    return nc


ROPE_THETA = 500000.0


def host_tables(q):
    f32 = np.float32
    inv_freq = (f32(ROPE_THETA) ** (-np.arange(0, 16, 2, dtype=f32) / f32(16))).astype(f32)
    rope = np.zeros((2, 128, TE), f32)
    kbA = np.zeros((128, 3, TE // 128), f32)
    for p, n in enumerate(PIECES):
        ext = n + 2 * HALO
        pos = (q * n - HALO + np.arange(ext)).astype(f32)
        ang = pos[:, None] * inv_freq[None, :]
        cos = np.cos(ang).astype(f32).T
        sin = np.sin(ang).astype(f32).T
        c16 = np.concatenate([cos, cos], 0)
        s16 = np.concatenate([-sin, sin], 0)
        rope[0, :, EOFF[p]:EOFF[p] + ext] = np.tile(c16, (8, 1))
        rope[1, :, EOFF[p]:EOFF[p] + ext] = np.tile(s16, (8, 1))
        seqlen = 4 * n
        for g, d in enumerate((1, 4, 16)):
            for idx in range(ext // 128):
                j, r = idx // d, idx % d
                te = d * (128 * j + np.arange(128)) + r
                ps = q * n - HALO + te
                kbA[:, g, EOFF[p] // 128 + idx] = np.where((ps >= 0) & (ps < seqlen), 1.0, 0.0)
    kl = np.arange(128)[:, None]
    ql = np.arange(128)[None, :]
    bmA = np.concatenate([(kl >= ql), (kl <= ql)], 1).astype(f32).astype(ml_dtypes.bfloat16)
    nrows = 32
    rows_seq = 4 * nrows
    mB = np.zeros((128, 27, 128), f32)
    kr_l, kc = np.arange(128)[:, None] // 64, np.arange(128)[:, None] % 64
    qr_l, qc = np.arange(128)[None, :] // 64, np.arange(128)[None, :] % 64
    cs = np.clip(qc - 8, 0, 48)
    colv = (kc >= cs) & (kc < cs + 16)
    for jt in range(5):
        b0, ntl = JT_TILES[jt]
        r = {0: 8, 1: 0, 2: 2, 3: nrows - 4, 4: nrows - 2}[jt]
        for t in range(ntl):
            dlt = -8 + 2 * (b0 + t)
            qrow = q * nrows + r + qr_l
            krow = q * nrows + r + dlt + kr_l
            rs = np.clip(qrow - 4, 0, rows_seq - 8)
            rowv = (krow >= rs) & (krow < rs + 8)
            mB[:, JT_OFF[jt] + t, :] = (rowv & colv)
    vexp = np.ascontiguousarray(np.broadcast_to(kbA[:, :, :, None], (128, 3, TE // 128, 64))).astype(ml_dtypes.bfloat16)
    return dict(rope=rope, vexp=vexp, bmA=bmA, mB=mB.astype(ml_dtypes.bfloat16))


def gather_bias4(rpb):
    kr_l, kc = np.arange(128)[:, None] // 64, np.arange(128)[:, None] % 64
    qr_l, qc = np.arange(128)[None, :] // 64, np.arange(128)[None, :] % 64
    dc = np.clip(kc - qc + 15, 0, 30)
    out = np.zeros((8, 8, 128, 128), np.float32)
    for bi in range(8):
        dr = -8 + 2 * bi + kr_l - qr_l
        dri = np.clip(dr + 7, 0, 14)
        out[:, bi] = rpb[:, dri, dc]
    return out


def shard_inputs(inputs):
    xp = np.asarray(inputs["x_prompt"], np.float32)
    xs = np.asarray(inputs["x_sample"], np.float32)
    wnames = ("ffn1_w_gate", "ffn1_w_up", "ffn1_w_down", "ffn1_pre_g", "ffn1_post_g", "mix_pre_g", "mix_post_g",
              "w_in", "b_gate", "w_branch_a", "w_branch_b", "w_out", "ffn2_pre_g", "ffn2_post_g",
              "ffn2_w_gate", "ffn2_w_up", "ffn2_w_down")
    shared = {k: np.ascontiguousarray(np.asarray(inputs[k], np.float32)[0]) for k in wnames}
    shared["bias4"] = gather_bias4(np.asarray(inputs["rpb"], np.float32)[0])
    tabs = [host_tables(q) for q in range(4)]
    maps = []
    for c in range(NCORES):
        b, q = c // 4, c % 4
        xe = np.zeros((TE, D), np.float32)
        for p, (src, n) in enumerate(((xp[b], PIECES[0]), (xs[b], PIECES[1]))):
            Sq = src.shape[0]
            lo = q * n - HALO
            hi = q * n + n + HALO
            a, bnd = max(lo, 0), min(hi, Sq)
            xe[EOFF[p] + (a - lo):EOFF[p] + (bnd - lo)] = src[a:bnd]
        m = {"xe": xe}
        m.update(shared)
        m.update(tabs[q])
        maps.append(m)
    return maps


_NC_CACHE = {}


def kernel(**inputs):
    maps = shard_inputs(inputs)
    if "nc" not in _NC_CACHE:
        _NC_CACHE["nc"] = build_program()
    res = run_bass_kernel_spmd(_NC_CACHE["nc"], maps, core_ids=list(range(NCORES)))
    B, SQ = np.asarray(inputs["x_prompt"]).shape[:2]
    DB, DS = np.asarray(inputs["x_sample"]).shape[:2]
    yp = np.zeros((B, SQ, D), np.float32)
    ys = np.zeros((DB, DS, D), np.float32)
    for c in range(NCORES):
        b, q = c // 4, c % 4
        y = np.asarray(res.results[c]["yout"], np.float32)
        yp[b, q * PIECES[0]:(q + 1) * PIECES[0]] = y[0:PIECES[0]]
        ys[b, q * PIECES[1]:(q + 1) * PIECES[1]] = y[PIECES[0]:NO]
    return (yp, ys)
```
